# Optimizing a Trainium2 kernel written in Bass

```python
import math, functools
import jax, jax.numpy as jnp
from jax import lax
import numpy as np

D_MODEL = 1024
BATCH = 16
SEQ = 256
DEPTH = 2
DEC_BATCH = 4
DEC_SEQ = 4096
PAST_LEN = 512

GRID_W = 64
N_MOD = 9
D_FF = 2816
EPS = 1e-6
ROPE_BASE = 10000.0
MLA_HEADS = 8
MLA_Q_LORA = 256
MLA_KV_LORA = 128
MLA_NOPE = 64
MLA_ROPE = 32
MLA_V = 64
MLA_QK = MLA_NOPE + MLA_ROPE
ATTN_Q_BLOCK = 128
RET_HEADS = 4
RET_DK = 128
RET_DV = 128
RET_CHUNK = 128
POOL_WINDOWS = (2, 4, 8, 16)
POOL_GROUPS = 4
POOL_GROUP_W = 128
POOL_W = POOL_GROUPS * POOL_GROUP_W
IN_SPLITS = (D_MODEL, D_MODEL, D_MODEL, MLA_Q_LORA, MLA_KV_LORA, MLA_ROPE,
             RET_HEADS * RET_DK, RET_HEADS * RET_DK, RET_HEADS * RET_DV, RET_HEADS * RET_DV, POOL_W)
IN_W = sum(IN_SPLITS)

kernel_name = 'hybrid_diffusion_mla_retention_pool_step'


def rms_norm(x, gain=None):
    xf = x.astype(jnp.float32)
    y = xf * lax.rsqrt(jnp.mean(xf * xf, axis=-1, keepdims=True) + EPS)
    if gain is not None:
        y = y * gain.astype(jnp.float32)
    return y.astype(x.dtype)


def rope(x, pos):
    d = x.shape[-1]
    half = d // 2
    inv = ROPE_BASE ** (-jnp.arange(half, dtype=jnp.float32) / half)
    ang = pos.astype(jnp.float32)[:, None] * inv[None, :]
    shape = (1, pos.shape[0]) + (1,) * (x.ndim - 3) + (half,)
    cos = jnp.cos(ang).reshape(shape)
    sin = jnp.sin(ang).reshape(shape)
    xf = x.astype(jnp.float32)
    x1, x2 = xf[..., :half], xf[..., half:]
    return jnp.concatenate([x1 * cos - x2 * sin, x1 * sin + x2 * cos], axis=-1).astype(x.dtype)


def axial_rope(x, row, col):
    half = x.shape[-1] // 2
    return jnp.concatenate([rope(x[..., :half], row), rope(x[..., half:], col)], axis=-1)


def grid_positions(n):
    rows = n // GRID_W
    row = jnp.repeat(jnp.arange(rows), GRID_W)
    col = jnp.arange(rows * GRID_W) % GRID_W
    return row, col


def split_in(z):
    offs = np.cumsum(IN_SPLITS)[:-1].tolist()
    return jnp.split(z, offs, axis=-1)


def swiglu(h, w_gu, w_down):
    g, u = jnp.split(h @ w_gu, 2, axis=-1)
    return (jax.nn.silu(g) * u) @ w_down


def block_attention(q, k, v):
    B, n, H, dq = q.shape
    nb = n // ATTN_Q_BLOCK
    qb = q.reshape(B, nb, ATTN_Q_BLOCK, H, dq).swapaxes(0, 1)
    scale = dq ** -0.5

    def one(qblk):
        s = jnp.einsum('bqhd,bkhd->bhqk', qblk, k).astype(jnp.float32) * scale
        p = jax.nn.softmax(s, axis=-1).astype(v.dtype)
        return jnp.einsum('bhqk,bkhe->bqhe', p, v)

    o = lax.map(one, qb)
    return o.swapaxes(0, 1).reshape(B, n, H, v.shape[-1])


def mla_queries(qc, q_norm, w_uq):
    B, n = qc.shape[:2]
    return (rms_norm(qc, q_norm) @ w_uq).reshape(B, n, MLA_HEADS, MLA_QK)


def mla_keys_values(ckv, kr, w_ukv):
    B, n = ckv.shape[:2]
    kv = (ckv @ w_ukv).reshape(B, n, MLA_HEADS, MLA_NOPE + MLA_V)
    k = jnp.concatenate([kv[..., :MLA_NOPE],
                         jnp.broadcast_to(kr[:, :, None, :], (B, n, MLA_HEADS, MLA_ROPE)).astype(kv.dtype)], axis=-1)
    return k, kv[..., MLA_NOPE:]


def retention_heads(rq, rk, rv):
    B, n = rq.shape[:2]
    q = rq.reshape(B, n, RET_HEADS, RET_DK).astype(jnp.float32)
    k = rk.reshape(B, n, RET_HEADS, RET_DK).astype(jnp.float32) * RET_DK ** -0.5
    v = rv.reshape(B, n, RET_HEADS, RET_DV).astype(jnp.float32)
    return q, k, v


def retention_scan(q, k, v, log_gamma, state0):
    B, n, H, _ = q.shape
    nc = n // RET_CHUNK

    def chunks(a):
        return a.reshape(B, nc, RET_CHUNK, H, a.shape[-1]).swapaxes(0, 1)

    idx = jnp.arange(RET_CHUNK, dtype=jnp.float32)
    diff = idx[:, None] - idx[None, :]
    decay = jnp.where(diff >= 0, jnp.exp(jnp.maximum(diff, 0.0)[None] * log_gamma[:, None, None]), 0.0)
    q_decay = jnp.exp((idx[:, None] + 1.0) * log_gamma[None, :])
    k_decay = jnp.exp((RET_CHUNK - 1.0 - idx[:, None]) * log_gamma[None, :])
    chunk_decay = jnp.exp(RET_CHUNK * log_gamma)

    def step(S, blk):
        qc, kc, vc = blk
        s = jnp.einsum('bihd,bjhd->bhij', qc, kc) * decay[None]
        inner = jnp.einsum('bhij,bjhe->bihe', s, vc)
        cross = jnp.einsum('bihd,bhde->bihe', qc, S) * q_decay[None, :, :, None]
        S = S * chunk_decay[None, :, None, None] + jnp.einsum('bjhd,bjhe->bhde', kc * k_decay[None, :, :, None], vc)
        return S, inner + cross

    S, out = lax.scan(step, state0, (chunks(q), chunks(k), chunks(v)))
    return out.swapaxes(0, 1).reshape(B, n, H, v.shape[-1]), S


def bidir_retention(q, k, v, ret_decay, state0):
    log_gamma = -jnp.exp(ret_decay.astype(jnp.float32))
    state0 = state0.astype(jnp.float32)
    o_f, s_f = retention_scan(q, k, v, log_gamma[0], state0[:, 0])
    o_b, s_b = retention_scan(jnp.flip(q, 1), jnp.flip(k, 1), jnp.flip(v, 1), log_gamma[1], state0[:, 1])
    return o_f + jnp.flip(o_b, 1), jnp.stack([s_f, s_b], axis=1)


def retention_out(o, rg):
    B, n = o.shape[:2]
    o = o * lax.rsqrt(jnp.mean(o * o, axis=-1, keepdims=True) + EPS)
    return jax.nn.silu(rg) * o.reshape(B, n, RET_HEADS * RET_DV).astype(rg.dtype)


def multiscale_pool(u, pool_w, pool_scale):
    B, n, _ = u.shape
    ug = u.reshape(B, n, POOL_GROUPS, POOL_GROUP_W).astype(jnp.float32)
    cs = jnp.concatenate([jnp.zeros_like(ug[:, :1]), jnp.cumsum(ug, axis=1)], axis=1)
    t = jnp.arange(n)[:, None]
    half = jnp.array(POOL_WINDOWS, dtype=jnp.int32)[None, :] // 2
    lo = jnp.clip(t - half, 0, n)
    hi = jnp.clip(t + half, 0, n)
    grp = jnp.arange(POOL_GROUPS)[None, :]
    win_sum = cs[:, hi, grp] - cs[:, lo, grp]
    mean = win_sum / (hi - lo).astype(jnp.float32)[None, :, :, None]
    y = (mean - ug).astype(u.dtype)
    y = jnp.einsum('bngc,gcd->bngd', y, pool_w).reshape(B, n, POOL_W)
    return y * pool_scale


def merge_branches(ga, gb, gc, oa, ob, oc, lp):
    B, n = oa.shape[:2]
    y = (jax.nn.sigmoid(ga) * (oa.reshape(B, n, MLA_HEADS * MLA_V) @ lp['w_branch_attn'])
         + jax.nn.sigmoid(gb) * (ob @ lp['w_branch_ret'])
         + jax.nn.sigmoid(gc) * (oc @ lp['w_branch_pool']))
    return y @ lp['w_out']


def mixer_context(h, lp):
    B, n, _ = h.shape
    ga, gb, gc, qc, kvc, kr, rq, rk, rv, rg, pu = split_in(h @ lp['w_in'])
    q = mla_queries(qc, lp['mla_q_norm'], lp['mla_w_uq'])
    ckv = rms_norm(kvc, lp['mla_kv_norm'])
    k, v = mla_keys_values(ckv, kr, lp['mla_w_ukv'])
    oa = block_attention(q, k, v)
    qr, krr, vr = retention_heads(rq, rk, rv)
    zero = jnp.zeros((B, 2, RET_HEADS, RET_DK, RET_DV), jnp.float32)
    o_r, st = bidir_retention(qr, krr, vr, lp['ret_decay'], zero)
    ob = retention_out(o_r, rg)
    oc = multiscale_pool(pu, lp['pool_w'], lp['pool_scale'])
    return merge_branches(ga, gb, gc, oa, ob, oc, lp), (ckv, kr, st)


def mixer_latent(h, ckv_ctx, kr_ctx, st_ctx, lp):
    B, n, _ = h.shape
    row, col = grid_positions(n)
    t = jnp.arange(n)
    ga, gb, gc, qc, kvc, kr, rq, rk, rv, rg, pu = split_in(h @ lp['w_in'])
    q = mla_queries(qc, lp['mla_q_norm'], lp['mla_w_uq'])
    q = jnp.concatenate([q[..., :MLA_NOPE], axial_rope(q[..., MLA_NOPE:], row, col)], axis=-1)
    k_lat, v_lat = mla_keys_values(rms_norm(kvc, lp['mla_kv_norm']), axial_rope(kr, row, col), lp['mla_w_ukv'])
    k_ctx, v_ctx = mla_keys_values(ckv_ctx.astype(h.dtype), kr_ctx.astype(h.dtype), lp['mla_w_ukv'])
    oa = block_attention(q, jnp.concatenate([k_lat, k_ctx], axis=1), jnp.concatenate([v_lat, v_ctx], axis=1))
    qr, krr, vr = retention_heads(rq, rk, rv)
    o_r, _ = bidir_retention(rope(qr, t), rope(krr, t), vr, lp['ret_decay'], st_ctx)
    ob = retention_out(o_r, rg)
    oc = multiscale_pool(pu, lp['pool_w'], lp['pool_scale'])
    return merge_branches(ga, gb, gc, oa, ob, oc, lp), ()


def trunk_layer(x, cvec, lp, mixer):
    mod = (jax.nn.silu(cvec) @ lp['w_ada'] + lp['b_ada']).reshape(cvec.shape[0], 1, N_MOD, D_MODEL)
    h = rms_norm(x, lp['norm_pre'][0]) * (1 + mod[:, :, 1]) + mod[:, :, 0]
    x = x + 0.5 * mod[:, :, 2] * rms_norm(swiglu(h, lp['ffn1_w_gu'], lp['ffn1_w_down']), lp['norm_post'][0])
    h = rms_norm(x, lp['norm_pre'][1]) * (1 + mod[:, :, 4]) + mod[:, :, 3]
    y, ctx_out = mixer(h)
    x = x + mod[:, :, 5] * rms_norm(y, lp['norm_post'][1])
    h = rms_norm(x, lp['norm_pre'][2]) * (1 + mod[:, :, 7]) + mod[:, :, 6]
    x = x + 0.5 * mod[:, :, 8] * rms_norm(swiglu(h, lp['ffn2_w_gu'], lp['ffn2_w_down']), lp['norm_post'][2])
    return x, ctx_out


def setup_inputs(seed: int = 0) -> dict:
    key = jax.random.key(seed)
    ks = jax.random.split(key, 27)
    f32 = jnp.float32
    L, D = DEPTH, D_MODEL

    def nrm(i, shape, scale):
        return scale * jax.random.normal(ks[i], shape, f32)

    ret_base = -(5.0 + jnp.arange(RET_HEADS, dtype=f32)) * math.log(2.0)
    return {
        'x_prompt': nrm(0, (BATCH, SEQ, D), 1.0),
        'x_sample': nrm(1, (DEC_BATCH, DEC_SEQ, D), 1.0),
        'cache_mla_ckv': nrm(2, (DEC_BATCH, L, PAST_LEN, MLA_KV_LORA), 1.0),
        'cache_mla_krope': nrm(3, (DEC_BATCH, L, PAST_LEN, MLA_ROPE), 1.0),
        'state_ret': nrm(4, (DEC_BATCH, L, 2, RET_HEADS, RET_DK, RET_DV), 0.5),
        'c': nrm(5, (DEC_BATCH, D), 1.0),
        'c_ctx': nrm(6, (D,), 1.0),
        'w_ada': nrm(7, (L, D, N_MOD * D), 0.5 * D ** -0.5),
        'b_ada': nrm(8, (L, N_MOD * D), 0.02),
        'norm_pre': 1.0 + nrm(9, (L, 3, D), 0.1),
        'norm_post': 1.0 + nrm(10, (L, 3, D), 0.1),
        'ffn1_w_gu': nrm(11, (L, D, 2 * D_FF), D ** -0.5),
        'ffn1_w_down': nrm(12, (L, D_FF, D), D_FF ** -0.5),
        'ffn2_w_gu': nrm(13, (L, D, 2 * D_FF), D ** -0.5),
        'ffn2_w_down': nrm(14, (L, D_FF, D), D_FF ** -0.5),
        'w_in': nrm(15, (L, D, IN_W), D ** -0.5),
        'mla_q_norm': 1.0 + nrm(16, (L, MLA_Q_LORA), 0.1),
        'mla_w_uq': nrm(17, (L, MLA_Q_LORA, MLA_HEADS * MLA_QK), MLA_Q_LORA ** -0.5),
        'mla_kv_norm': 1.0 + nrm(18, (L, MLA_KV_LORA), 0.1),
        'mla_w_ukv': nrm(19, (L, MLA_KV_LORA, MLA_HEADS * (MLA_NOPE + MLA_V)), MLA_KV_LORA ** -0.5),
        'ret_decay': ret_base + nrm(20, (L, 2, RET_HEADS), 0.05),
        'pool_w': nrm(21, (L, POOL_GROUPS, POOL_GROUP_W, POOL_GROUP_W), POOL_GROUP_W ** -0.5),
        'pool_scale': 1.0 + nrm(22, (L, POOL_W), 0.1),
        'w_branch_attn': nrm(23, (L, MLA_HEADS * MLA_V, D), (MLA_HEADS * MLA_V) ** -0.5),
        'w_branch_ret': nrm(24, (L, RET_HEADS * RET_DV, D), (RET_HEADS * RET_DV) ** -0.5),
        'w_branch_pool': nrm(25, (L, POOL_W, D), POOL_W ** -0.5),
        'w_out': nrm(26, (L, D, D), D ** -0.5),
    }


def reference(x_prompt, x_sample, cache_mla_ckv, cache_mla_krope, state_ret, c, c_ctx,
              w_ada, b_ada, norm_pre, norm_post, ffn1_w_gu, ffn1_w_down, ffn2_w_gu, ffn2_w_down,
              w_in, mla_q_norm, mla_w_uq, mla_kv_norm, mla_w_ukv, ret_decay, pool_w, pool_scale,
              w_branch_attn, w_branch_ret, w_branch_pool, w_out):
    x_ctx, x_lat = x_prompt, x_sample
    ckv_list, kr_list, st_list = [], [], []
    for l in range(DEPTH):
        lp = dict(w_ada=w_ada[l], b_ada=b_ada[l], norm_pre=norm_pre[l], norm_post=norm_post[l],
                  ffn1_w_gu=ffn1_w_gu[l], ffn1_w_down=ffn1_w_down[l],
                  ffn2_w_gu=ffn2_w_gu[l], ffn2_w_down=ffn2_w_down[l], w_in=w_in[l],
                  mla_q_norm=mla_q_norm[l], mla_w_uq=mla_w_uq[l], mla_kv_norm=mla_kv_norm[l],
                  mla_w_ukv=mla_w_ukv[l], ret_decay=ret_decay[l], pool_w=pool_w[l],
                  pool_scale=pool_scale[l], w_branch_attn=w_branch_attn[l],
                  w_branch_ret=w_branch_ret[l], w_branch_pool=w_branch_pool[l], w_out=w_out[l])
        x_ctx, (ckv_l, kr_l, st_l) = trunk_layer(x_ctx, c_ctx[None, :], lp,
                                                  functools.partial(mixer_context, lp=lp))
        ckv_list.append(ckv_l)
        kr_list.append(kr_l)
        st_list.append(st_l)
        x_lat, _ = trunk_layer(x_lat, c, lp,
                               functools.partial(mixer_latent, ckv_ctx=cache_mla_ckv[:, l],
                                                 kr_ctx=cache_mla_krope[:, l], st_ctx=state_ret[:, l], lp=lp))
    new_mla_ckv = jnp.stack(ckv_list, axis=1)
    new_mla_krope = jnp.stack(kr_list, axis=1)
    new_state_ret = jnp.stack(st_list, axis=1)
    return (x_ctx, x_lat, new_mla_ckv, new_mla_krope, new_state_ret)
```

```python
import math
import os
import numpy as np
RET_STOP = float(os.environ.get('RET_STOP', '99'))
from contextlib import ExitStack
import concourse.bass as bass
import concourse.mybir as mybir
from concourse.bass_utils import run_bass_kernel_spmd

F32 = mybir.dt.float32
BF16 = mybir.dt.bfloat16
AF = mybir.ActivationFunctionType
ALU = mybir.AluOpType

D = 1024
L = 2
DFF = 2816
TL = 4096
TP = 512
TT = TL + TP
PAST = 512
EPS = 1e-6
INW = 6048
O_GA, O_GB, O_GC, O_QC, O_KVC, O_KR, O_RQ, O_RK, O_RV, O_RG, O_PU = 0, 1024, 2048, 3072, 3328, 3456, 3488, 4000, 4512, 5024, 5536
NCORES = 8
SLOT = 5632
NSLOT = 3

V_BADA, V_NPRE, V_NPOST, V_QN, V_KVN, V_PS = 0, 72, 96, 120, 122, 123
NVEC = 127


class Tk:
    __slots__ = ("name", "w", "r", "sem", "cnt", "persist")

    def __init__(self, name, persist=False):
        self.persist = persist
        self.name = name
        self.w = {}
        self.r = {}
        self.sem = None
        self.cnt = 0


class KL(list):
    pass


def _flat(tks):
    out = []
    for t in tks:
        if isinstance(t, (list, tuple)):
            out.extend(_flat(t))
        else:
            out.append(t)
    return out


class Trk:
    ENG = ("pe", "act", "dve", "pool", "sp")

    def __init__(self, nc, es):
        self.nc = nc
        self.es = es
        self.e = {"pe": nc.tensor, "act": nc.scalar, "dve": nc.vector, "pool": nc.gpsimd, "sp": nc.sync}
        self.sem = {k: es.enter_context(nc.semaphore("s_" + k)) for k in self.ENG}
        self.seq = {k: 0 for k in self.ENG}
        self.seen = {k: {} for k in self.ENG}
        self.semobj = dict(self.sem)
        self.nsem = 0
        self.ninst = 0
        self.pool_sems = []
        self.pool_cnt = {}
        self.free_keys = {"sp": [], "pool": []}
        self.phase_keys = []
        self.key_q = {}

    NPOOL = 90

    def _dsem(self, tk, eng):
        if tk.sem is None:
            tk.sem = {}
        if eng not in tk.sem:
            if len(self.pool_sems) < self.NPOOL:
                h = self.es.enter_context(self.nc.semaphore("d%d" % len(self.pool_sems)))
                key = ("d", len(self.pool_sems))
                self.pool_sems.append(key)
                self.semobj[key] = h
                self.pool_cnt[key] = 0
                self.key_q[key] = eng
            elif self.free_keys[eng]:
                key = self.free_keys[eng].pop()
            else:
                cands = [k for k in self.pool_sems if self.key_q[k] == eng]
                key = cands[self.nsem % len(cands)]
            self.nsem += 1
            tk.sem[eng] = key
            if not getattr(tk, "persist", False):
                self.phase_keys.append(key)
        return tk.sem[eng]

    def end_phase(self):
        for k in self.phase_keys:
            q = self.key_q[k]
            if k not in self.free_keys[q]:
                self.free_keys[q].append(k)
        self.phase_keys = []

    def _wait(self, eng, deps):
        seen = self.seen[eng]
        E = self.e[eng]
        for k, v in deps.items():
            if eng == "pe" and k == "pe":
                continue
            if k in self.pool_cnt:
                v = self.pool_cnt[k]
            if seen.get(k, 0) < v:
                E.wait_ge(self.semobj[k], v)
                seen[k] = v
                self.ninst += 1

    def _deps(self, reads, writes):
        d = {}
        for t in reads:
            for k, v in t.w.items():
                if d.get(k, 0) < v:
                    d[k] = v
        for t in writes:
            for k, v in t.w.items():
                if d.get(k, 0) < v:
                    d[k] = v
            for k, v in t.r.items():
                if d.get(k, 0) < v:
                    d[k] = v
        return d

    def op(self, eng, fn, reads=(), writes=(), inc=True):
        reads = _flat(reads)
        writes = _flat(writes)
        self._wait(eng, self._deps(reads, writes))
        ins = fn(self.e[eng])
        self.ninst += 1
        v = self.seq[eng] + 1
        if inc:
            ins.then_inc(self.sem[eng], 1)
            self.seq[eng] = v
        for t in writes:
            if t.w.get(eng, 0) < v:
                t.w[eng] = v
        for t in reads:
            if t.r.get(eng, 0) < v:
                t.r[eng] = v
        return ins

    def dma(self, eng, out, in_, reads, write):
        reads = _flat(reads)
        wl = _flat([write])
        write = wl[0]
        self._wait(eng, self._deps(reads, wl))
        k = self._dsem(write, eng)
        self.e[eng].dma_start(out=out, in_=in_).then_inc(self.semobj[k], 16)
        self.ninst += 1
        self.pool_cnt[k] += 16
        v = self.pool_cnt[k]
        for t in wl:
            t.w[k] = v
        for t in reads:
            t.r[k] = v

    def barrier(self, tks=()):
        for eng in self.ENG:
            d = {k: self.seq[k] for k in self.ENG if k != eng}
            for k, v in self.pool_cnt.items():
                d[k] = v
            self._wait(eng, d)


def build_program(debug_stop=None):
    nc = bass.Bass("TRN2", target_bir_lowering=False)
    es = ExitStack()
    with es:
        T = Trk(nc, es)

        def din(name, shape, dt=F32):
            return nc.dram_tensor(name, list(shape), dt, kind="ExternalInput").ap()

        def dout(name, shape, dt=F32):
            return nc.dram_tensor(name, list(shape), dt, kind="ExternalOutput").ap()

        def dscr(name, shape, dt):
            return nc.dram_tensor(name, list(shape), dt, kind="Internal").ap()

        xl = din("xl", [TL, D])
        xp = din("xp", [TP, D])
        cckv = din("cckv", [L, PAST, 128])
        ckr = din("ckr", [L, PAST, 32])
        st0 = din("st0", [L, 2, 4, 128, 128])
        cv = din("cv", [16, 128])
        vecs = din("vecs", [L, NVEC, 128])
        kvn_row = din("kvn_row", [L, 128])
        ret_decay = din("ret_decay", [L, 8])
        w_ada = din("w_ada", [L, D, 9 * D])
        w_gu = [din("ffn1_w_gu", [L, D, 2 * DFF]), din("ffn2_w_gu", [L, D, 2 * DFF])]
        w_dn = [din("ffn1_w_down", [L, DFF, D]), din("ffn2_w_down", [L, DFF, D])]
        w_in = din("w_in", [L, D, INW])
        w_uq = din("mla_w_uq", [L, 256, 768])
        w_ukv = din("mla_w_ukv", [L, 128, 1024])
        pool_w = din("pool_w", [L, 512, 128])
        w_ba = din("w_branch_attn", [L, 512, D])
        w_br = din("w_branch_ret", [L, 512, D])
        w_bp = din("w_branch_pool", [L, 512, D])
        w_out = din("w_out", [L, D, D])
        c_ident = din("c_ident", [128, 128])
        c_cosR = din("c_cosR", [128, TT])
        c_sinR = din("c_sinR", [128, TT])
        c_cosM = din("c_cosM", [128, TT])
        c_sinM = din("c_sinM", [128, TT])
        c_ret = din("c_ret", [6, 128, 512])
        c_kj = din("c_kj", [128, 2])
        c_rcnt = din("c_rcnt", [4, TT])

        yl = dout("yl", [TL, D])
        yp = dout("yp", [TP, D])
        ockv = dout("ockv", [2, L, 256, 128])
        okr = dout("okr", [2, L, 256, 32])
        ost = dout("ost", [2, L, 2, 4, 128, 128])

        gu_b = [[dscr("gu_b%d_%d" % (f, l), [D, 2 * DFF], BF16) for l in range(L)] for f in range(2)]
        dn_b = [[dscr("dn_b%d_%d" % (f, l), [DFF, D], BF16) for l in range(L)] for f in range(2)]
        in_b = [dscr("in_b%d" % l, [D, INW], BF16) for l in range(L)]
        inr_b = [dscr("inr_b%d" % l, [D, 1056], BF16) for l in range(L)]
        uq_b = [dscr("uq_b%d" % l, [256, 768], BF16) for l in range(L)]
        uqr_b = [dscr("uqr_b%d" % l, [256, 768], BF16) for l in range(L)]
        ukv_b = [dscr("ukv_b%d" % l, [128, 1024], BF16) for l in range(L)]
        pw_b = [dscr("pw_b%d" % l, [512, 128], BF16) for l in range(L)]
        ba_b = [dscr("ba_b%d" % l, [512, D], BF16) for l in range(L)]
        br_b = [dscr("br_b%d" % l, [512, D], BF16) for l in range(L)]
        bp_b = [dscr("bp_b%d" % l, [512, D], BF16) for l in range(L)]
        out_b = [dscr("out_b%d" % l, [D, D], BF16) for l in range(L)]
        XT1 = dscr("XT1", [8, 128, TT], F32)
        X1T = [dscr("X1T%d" % l, [8, 128, TT], F32) for l in range(L)]
        SG = [dscr("SG%d" % l, [24, 128, TT], BF16) for l in range(L)]
        QCN = [dscr("QCN%d" % l, [2, 128, TT], BF16) for l in range(L)]
        CKVT = [dscr("CKVT%d" % l, [128, TT], BF16) for l in range(L)]
        KRT = [dscr("KRT%d" % l, [32, TT], BF16) for l in range(L)]
        RQT = [dscr("RQT%d" % l, [4, 128, TT], BF16) for l in range(L)]
        RKT = [dscr("RKT%d" % l, [4, 128, TT], BF16) for l in range(L)]
        RGS = [dscr("RGS%d" % l, [4, 128, TT], BF16) for l in range(L)]
        RV = [dscr("RV%d" % l, [4, 128, TT // 128, 128], BF16) for l in range(L)]
        PUT = [dscr("PUT%d" % l, [4, 128, TT], F32) for l in range(L)]
        OA = [dscr("OA%d" % l, [512, TT], BF16) for l in range(L)]
        OB = [dscr("OB%d" % l, [4, 128, TT], BF16) for l in range(L)]
        OC = [dscr("OC%d" % l, [4, 128, TT], BF16) for l in range(L)]
        tk = {}

        def TKn(name):
            if name not in tk:
                tk[name] = Tk(name, True)
            return tk[name]

        def sb(name, shape, dt):
            return es.enter_context(nc.sbuf_tensor(name, list(shape), dt))

        ident = sb("ident", [128, 128], F32)
        ones_f = sb("ones_f", [128, 128], F32)
        ones_b = sb("ones_b", [128, 128], BF16)
        MV = sb("MV", [128, L * 2 * 3 * 3 * 8], F32)
        VT = sb("VT", [128, L * 128], F32)
        kvn_bc = sb("kvn_bc", [128, L * 128], F32)
        ring = [sb("ring%d" % i, [128, SLOT], BF16) for i in range(NSLOT)]
        ring_tk = [Tk("ring%d" % i, True) for i in range(NSLOT)]
        ring_pos = [0]
        psb = [es.enter_context(nc.psum_tensor("psb%d" % i, [128, 512], F32)) for i in range(8)]
        ps_tk = [Tk("ps%d" % i) for i in range(8)]
        k_const = Tk("const", True)
        k_out = [Tk("o_yl", True), Tk("o_yp", True), Tk("o_ckv", True), Tk("o_kr", True), Tk("o_st", True)]

        def MVs(l, g, s, abc):
            o = (((l * 2 + g) * 3 + s) * 3 + abc) * 8
            return MV[:, o:o + 8]

        def vt(l, row0, n):
            return VT[:, l * 128 + row0: l * 128 + row0 + n]

        def conv(dst, src, rows, name, nsplit=1):
            k = TKn(name)
            step = rows // nsplit
            for i in range(nsplit):
                T.dma("pool", dst[i * step:(i + 1) * step, :], src[i * step:(i + 1) * step, :], [], k)
            return k

        def conv_layer(l):
            conv(gu_b[0][l], w_gu[0][l], D, "gu0_%d" % l, 4)
            conv(dn_b[0][l], w_dn[0][l], DFF, "dn0_%d" % l, 2)
            conv(in_b[l], w_in[l], D, "in_%d" % l, 4)
            k = TKn("inr_%d" % l)
            for (d0, s0, n) in [(0, O_KR + 8, 8), (8, O_KR, 8), (16, O_KR + 24, 8), (24, O_KR + 16, 8)]:
                T.dma("pool", inr_b[l][:, d0:d0 + n], w_in[l][:, s0:s0 + n], [], k)
            for base_d, base_s in [(32, O_RQ), (544, O_RK)]:
                srcv = w_in[l][:, base_s:base_s + 512].rearrange("k (h t d) -> k h t d", h=4, t=2)
                dstv = inr_b[l][:, base_d:base_d + 512].rearrange("k (h t d) -> k h t d", h=4, t=2)
                for h in range(4):
                    T.dma("pool", dstv[:, h, 0, :], srcv[:, h, 1, :], [], k)
                    T.dma("pool", dstv[:, h, 1, :], srcv[:, h, 0, :], [], k)
            conv(uq_b[l], w_uq[l], 256, "uq_%d" % l)
            k = TKn("uqr_%d" % l)
            sv = w_uq[l].rearrange("k (h c) -> k h c", h=8)
            dv = uqr_b[l].rearrange("k (h c) -> k h c", h=8)
            for (d0, s0) in [(64, 72), (72, 64), (80, 88), (88, 80)]:
                T.dma("pool", dv[:, :, d0:d0 + 8], sv[:, :, s0:s0 + 8], [], k)
            T.dma("pool", dv[:, :, 0:64], sv[:, :, 0:64], [], k)
            conv(ukv_b[l], w_ukv[l], 128, "ukv_%d" % l)
            conv(pw_b[l], pool_w[l], 512, "pw_%d" % l)
            conv(ba_b[l], w_ba[l], 512, "ba_%d" % l)
            conv(br_b[l], w_br[l], 512, "br_%d" % l)
            conv(bp_b[l], w_bp[l], 512, "bp_%d" % l)
            conv(out_b[l], w_out[l], D, "out_%d" % l, 2)
            conv(gu_b[1][l], w_gu[1][l], D, "gu1_%d" % l, 4)
            conv(dn_b[1][l], w_dn[1][l], DFF, "dn1_%d" % l, 2)

        T.dma("sp", ident[:], c_ident[:, :], [], k_const)
        T.op("dve", lambda E: E.memset(ones_f[:], 1.0), [], [k_const])
        T.op("dve", lambda E: E.memset(ones_b[:], 1.0), [], [k_const])

        def ring_load(src_aps, reads):
            i = ring_pos[0] % NSLOT
            ring_pos[0] += 1
            for ap_, off, kc, ncols in src_aps:
                dst = ring[i][:, off:off + kc * ncols].rearrange("p (k n) -> p k n", k=kc)
                T.dma("sp", dst, ap_, reads, ring_tk[i])
            return i

        def wsrc(Wb, k0, kc, c0, ncols):
            return Wb[k0 * 128:(k0 + kc) * 128, c0:c0 + ncols].rearrange("(k p) n -> p k n", p=128)

        def slot_view(i, off, kc, ncols):
            return ring[i][:, off:off + kc * ncols].rearrange("p (k n) -> p k n", k=kc)

        def mm_group(banks, M, lhsT_fn, rhs_fn, kc, nsub, reads, sub=512, first=True, last=True):
            for k in range(kc):
                rk = [x[k] if isinstance(x, KL) else x for x in reads]
                for s in range(nsub):
                    b = banks[s]
                    T.op("pe", lambda E, k=k, s=s, b=b: E.matmul(
                        psb[b][0:M, 0:sub], lhsT=lhsT_fn(k), rhs=rhs_fn(k, s),
                        start=(first and k == 0), stop=(last and k == kc - 1)),
                        rk, [ps_tk[b]], inc=(k == kc - 1))

        def ps2(banks, M, n):
            out = []
            for s, b in enumerate(banks):
                w = min(512, n - s * 512)
                if w <= 0:
                    break
                out.append((psb[b][0:M, 0:w], ps_tk[b], s * 512, w))
            return out

        with ExitStack() as es0:
            def sb0(name, shape, dt):
                return es0.enter_context(nc.sbuf_tensor(name, list(shape), dt))
            vraw = sb0("vraw", [128, 128], F32)
            cvraw = sb0("cvraw", [16, 128], F32)
            scv = sb0("scv", [128, 16], BF16)
            wst = [sb0("wst%d" % j, [128, 4096], F32) for j in range(4)]
            wbf = [sb0("wbf%d" % j, [128, 4096], BF16) for j in range(2)]
            k_wst = [Tk("wst%d" % j) for j in range(4)]
            k_wbf = [Tk("wbf%d" % j) for j in range(2)]
            modT = sb0("modT", [128, 144], F32)
            k_vraw, k_cv, k_scv, k_modT, k_VT, k_MV = Tk("vraw"), Tk("cvraw"), Tk("scv"), Tk("modT"), TKn("VT"), TKn("MV")
            k_wada = Tk("wada")
            T.dma("sp", cvraw[:], cv[:, :], [], k_cv)
            T.op("pe", lambda E: E.transpose(psb[0][:, 0:16], cvraw[:], ident[0:16, 0:16]), [k_cv, k_const], [ps_tk[0]])
            T.op("act", lambda E: E.activation(out=scv[:], in_=psb[0][:, 0:16], func=AF.Silu), [ps_tk[0]], [k_scv])
            for l in range(L):
                T.op("dve", lambda E: E.memset(vraw[:], 0.0), [], [k_vraw])
                T.dma("sp", vraw[0:NVEC, :], vecs[l], [], k_vraw)
                T.op("pe", lambda E: E.transpose(psb[1][:, 0:128], vraw[:], ident[:]), [k_vraw, k_const], [ps_tk[1]])
                T.op("act", lambda E, l=l: E.copy(out=VT[:, l * 128:(l + 1) * 128], in_=psb[1][:, 0:128]), [ps_tk[1]], [k_VT])
                T.dma("sp", kvn_bc[:, l * 128:(l + 1) * 128], kvn_row[l:l + 1, :].partition_broadcast(128), [], k_const)
                for u in range(18):
                    i = u % 4
                    T.dma("sp", wst[i][:, :].rearrange("p (k n) -> p k n", k=8), wsrc(w_ada[l], 0, 8, u * 512, 512), [], k_wst[i])
                    i2 = u % 2
                    T.op("dve", lambda E, i=i, i2=i2: E.tensor_copy(out=wbf[i2][:, 0:2048], in_=wst[i][:, 0:2048]), [k_wst[i]], [k_wbf[i2]])
                    T.op("act", lambda E, i=i, i2=i2: E.copy(out=wbf[i2][:, 2048:4096], in_=wst[i][:, 2048:4096]), [k_wst[i]], [k_wbf[i2]])
                    rf = wbf[i2][:, :].rearrange("p (k n) -> p k n", k=8)
                    for c in range(4):
                        blk = u * 4 + c
                        for k in range(8):
                            T.op("pe", lambda E, k=k, c=c, blk=blk, rf=rf: E.matmul(
                                psb[2][:, blk * 2:blk * 2 + 2], lhsT=rf[:, k, c * 128:(c + 1) * 128],
                                rhs=scv[:, k:16:8], start=(k == 0), stop=(k == 7)),
                                [k_wbf[i2], k_scv], [ps_tk[2]], inc=(k == 7))
                T.op("act", lambda E: E.copy(out=modT[:], in_=psb[2][:, 0:144]), [ps_tk[2]], [k_modT])
                mT = modT[:].rearrange("p (b g) -> p b g", g=2)
                for g in range(2):
                    T.op("dve", lambda E, g=g, l=l: E.tensor_tensor(out=mT[:, :, g], in0=mT[:, :, g], in1=vt(l, V_BADA, 72), op=ALU.add),
                         [k_modT, k_VT], [k_modT])
                for g in range(2):
                    for s in range(3):
                        sh = mT[:, (3 * s) * 8:(3 * s) * 8 + 8, g]
                        sc = mT[:, (3 * s + 1) * 8:(3 * s + 1) * 8 + 8, g]
                        gt = mT[:, (3 * s + 2) * 8:(3 * s + 2) * 8 + 8, g]
                        T.op("dve", lambda E, sc=sc, l=l, g=g, s=s: E.scalar_tensor_tensor(
                            out=MVs(l, g, s, 0), in0=sc, scalar=1.0, in1=vt(l, V_NPRE + 8 * s, 8), op0=ALU.add, op1=ALU.mult),
                            [k_modT, k_VT], [k_MV])
                        T.op("dve", lambda E, sh=sh, l=l, g=g, s=s: E.tensor_copy(out=MVs(l, g, s, 1), in_=sh), [k_modT], [k_MV])
                        T.op("dve", lambda E, gt=gt, l=l, g=g, s=s: E.scalar_tensor_tensor(
                            out=MVs(l, g, s, 2), in0=gt, scalar=(1.0 if s == 1 else 0.5), in1=vt(l, V_NPOST + 8 * s, 8),
                            op0=ALU.mult, op1=ALU.mult), [k_modT, k_VT], [k_MV])
            T.barrier([k_const, k_vraw, k_cv] + ring_tk)
            T.end_phase()
        k_VT, k_MV = tk["VT"], tk["MV"]

        for l in range(L):
            conv_layer(l)

        STS = [(0, 1024, 0), (1024, 1024, 0), (2048, 1024, 0), (3072, 1024, 0), (4096, 512, 1)]

        def rms_stats(pieces_fn, nchunks, n, nfeat, rstd, k_rstd, sqt, k_sq, reads):
            nsub = (n + 511) // 512
            for c in range(nchunks):
                ap_, tk_ = pieces_fn(c)
                j = c % 2
                T.op("act", lambda E, ap_=ap_, j=j: E.activation(out=sqt[j][:, 0:n], in_=ap_, func=AF.Square), [tk_] + reads, [k_sq[j]])
                for s in range(nsub):
                    w = min(512, n - s * 512)
                    T.op("pe", lambda E, s=s, w=w, j=j, c=c: E.matmul(psb[s][:, 0:w], lhsT=ones_b[:], rhs=sqt[j][:, s * 512:s * 512 + w],
                                                                   start=(c == 0), stop=(c == nchunks - 1)),
                         [k_sq[j], k_const], [ps_tk[s]], inc=True)
            for s in range(nsub):
                w = min(512, n - s * 512)
                T.op("act", lambda E, s=s, w=w: E.activation(out=rstd[:, s * 512:s * 512 + w], in_=psb[s][:, 0:w], func=AF.Sqrt,
                                                           scale=1.0 / nfeat, bias=eps_t[:, 0:1]), [ps_tk[s], k_const], [k_rstd])
            T.op("dve", lambda E: E.reciprocal(out=rstd[:, 0:n], in_=rstd[:, 0:n]), [k_rstd], [k_rstd])

        eps_t = sb("eps_t", [128, 1], F32)
        T.op("dve", lambda E: E.memset(eps_t[:], EPS), [], [k_const])

        def run_rowlocal(l, phase, esx):
            def sbx(name, shape, dt):
                return esx.enter_context(nc.sbuf_tensor(name + "_%d%d" % (l, phase), list(shape), dt))
            xT = sbx("xT", [128, 8, 1024], F32)
            yT = sbx("yT", [128, 8, 1024], F32)
            hT = sbx("hT", [128, 8, 1024], BF16)
            aT = sbx("aT", [128, 22, 1024], BF16)
            rstd = sbx("rstd", [128, 1024], F32)
            sqt = [sbx("sq%d" % j, [128, 1024], BF16) for j in range(2)]
            tmp = [sbx("tmp%d" % j, [128, 1024], F32) for j in range(2)]
            stg = [sbx("stg%d" % j, [128, 1024], BF16) for j in range(3)]
            tab = [sbx("tab%d" % j, [128, 1024], F32) for j in range(2)]
            k_xT, k_yT, k_hT = KL(Tk("xT%d" % i) for i in range(8)), KL(Tk("yT%d" % i) for i in range(8)), KL(Tk("hT%d" % i) for i in range(8))
            k_aT, k_rstd = Tk("aT"), Tk("rstd")
            k_sq = [Tk("sq0"), Tk("sq1")]
            k_tmp = [Tk("tmp0"), Tk("tmp1")]
            k_stg = [Tk("stg%d" % j) for j in range(3)]
            k_tab = [Tk("tab%d" % j) for j in range(2)]
            cnt = {"tmp": 0, "stg": 0, "yb": 0}

            def nxt(key, n):
                i = cnt[key] % n
                cnt[key] += 1
                return i

            def prenorm(n, g, s):
                rms_stats(lambda c: (xT[:, c, 0:n], k_xT[c]), 8, n, D, rstd, k_rstd, sqt, k_sq, [])
                A = MVs(l, g, s, 0)
                B = MVs(l, g, s, 1)
                for c in range(8):
                    j = nxt("tmp", 2)
                    T.op("dve", lambda E, c=c, j=j: E.scalar_tensor_tensor(out=tmp[j][:, 0:n], in0=xT[:, c, 0:n], scalar=A[:, c:c + 1], in1=rstd[:, 0:n],
                                                                           op0=ALU.mult, op1=ALU.mult), [k_xT[c], k_rstd, k_MV], [k_tmp[j]])
                    T.op("act", lambda E, c=c, j=j: E.activation(out=hT[:, c, 0:n], in_=tmp[j][:, 0:n], func=AF.Identity, bias=B[:, c:c + 1]),
                         [k_tmp[j], k_MV], [k_hT[c]])

            def post_resid(n, g, s):
                rms_stats(lambda c: (yT[:, c, 0:n], k_yT[c]), 8, n, D, rstd, k_rstd, sqt, k_sq, [])
                C = MVs(l, g, s, 2)
                for c in range(8):
                    j = nxt("tmp", 2)
                    T.op("pool", lambda E, c=c, j=j: E.tensor_tensor(out=tmp[j][:, 0:n], in0=yT[:, c, 0:n], in1=rstd[:, 0:n], op=ALU.mult), [k_yT[c], k_rstd], [k_tmp[j]])
                    T.op("dve", lambda E, c=c, j=j: E.scalar_tensor_tensor(out=xT[:, c, 0:n], in0=tmp[j][:, 0:n], scalar=C[:, c:c + 1], in1=xT[:, c, 0:n],
                                                                           op0=ALU.mult, op1=ALU.add), [k_tmp[j], k_xT[c], k_MV], [k_xT[c]])

            def ffn(n, g, s, f):
                nsub = n // 512
                prenorm(n, g, s)
                Wg = gu_b[f][l]
                kW = tk["gu%d_%d" % (f, l)]
                for u in range(11):
                    i = ring_load([(wsrc(Wg, 0, 8, u * 256, 256), 0, 8, 256), (wsrc(Wg, 0, 8, DFF + u * 256, 256), 2048, 8, 256)], [kW])
                    gv = slot_view(i, 0, 8, 256)
                    uv = slot_view(i, 2048, 8, 256)
                    for jj in range(2):
                        j = u * 2 + jj
                        mm_group([0, 1], 128, lambda k: gv[:, k, jj * 128:(jj + 1) * 128], lambda k, s_: hT[:, k, s_ * 512:(s_ + 1) * 512], 8, nsub, [ring_tk[i], k_hT])
                        mm_group([2, 3], 128, lambda k: uv[:, k, jj * 128:(jj + 1) * 128], lambda k, s_: hT[:, k, s_ * 512:(s_ + 1) * 512], 8, nsub, [ring_tk[i], k_hT])
                        t = nxt("tmp", 2)
                        for s_ in range(nsub):
                            T.op("act", lambda E, s_=s_, t=t: E.activation(out=tmp[t][:, s_ * 512:(s_ + 1) * 512], in_=psb[s_][:, :], func=AF.Silu),
                                 [ps_tk[s_]], [k_tmp[t]])
                        for s_ in range(nsub):
                            T.op("dve", lambda E, s_=s_, t=t, j=j: E.tensor_tensor(out=aT[:, j, s_ * 512:(s_ + 1) * 512], in0=tmp[t][:, s_ * 512:(s_ + 1) * 512],
                                                                                   in1=psb[2 + s_][:, :], op=ALU.mult), [k_tmp[t], ps_tk[2 + s_]], [k_aT])
                Wd = dn_b[f][l]
                kWd = tk["dn%d_%d" % (f, l)]
                for u in range(4):
                    i = ring_load([(wsrc(Wd, 0, 22, u * 256, 256), 0, 22, 256)], [kWd])
                    dvw = slot_view(i, 0, 22, 256)
                    for jj in range(2):
                        dc = u * 2 + jj
                        yb = [4, 5] if nxt("yb", 2) == 0 else [6, 7]
                        mm_group(yb, 128, lambda k: dvw[:, k, jj * 128:(jj + 1) * 128], lambda k, s_: aT[:, k, s_ * 512:(s_ + 1) * 512], 22, nsub, [ring_tk[i], k_aT])
                        for s_ in range(nsub):
                            T.op("act", lambda E, s_=s_, dc=dc, b=yb[s_]: E.copy(out=yT[:, dc, s_ * 512:(s_ + 1) * 512], in_=psb[b][:, :]), [ps_tk[yb[s_]]], [k_yT[dc]])
                post_resid(n, g, s)

            def load_x(t0, n, g):
                if l == 0 and phase == 1:
                    src = xl if g == 0 else xp
                    r0 = t0 if g == 0 else t0 - TL
                    for blk in range(n // 128):
                        j = nxt("tmp", 2)
                        T.dma("sp", tmp[j][:, :], src[r0 + blk * 128:r0 + (blk + 1) * 128, :], [], k_tmp[j])
                        yb = [4, 5] if nxt("yb", 2) == 0 else [6, 7]
                        for c in range(8):
                            b = yb[c // 4]
                            T.op("pe", lambda E, c=c, b=b, j=j: E.transpose(psb[b][:, (c % 4) * 128:(c % 4 + 1) * 128], tmp[j][:, c * 128:(c + 1) * 128], ident[:]),
                                 [k_tmp[j], k_const], [ps_tk[b]])
                        for hh in range(2):
                            b = yb[hh]
                            T.op("act" if hh == 0 else "dve", lambda E, hh=hh, b=b, blk=blk: (E.copy if hh == 0 else E.tensor_copy)(
                                out=xT[:, hh * 4:(hh + 1) * 4, blk * 128:(blk + 1) * 128], in_=psb[b][:, :].rearrange("p (c t) -> p c t", c=4)),
                                [ps_tk[b]], [k_xT[hh * 4:(hh + 1) * 4]])
                else:
                    src = XT1 if phase == 1 else X1T[l]
                    ksrc = TKn("XT1") if phase == 1 else TKn("X1T%d" % l)
                    T.dma("sp", xT[:, :, 0:n], src[:, :, t0:t0 + n].rearrange("c p t -> p c t"), [ksrc], k_xT)

            def store_fm(dst3, kdst, t0, n):
                T.dma("pool", dst3[:, :, t0:t0 + n].rearrange("c p t -> p c t"), xT[:, :, 0:n], [k_xT], kdst)

            def store_final(t0, n, g):
                dst = yl if g == 0 else yp
                kd = k_out[0] if g == 0 else k_out[1]
                r0 = t0 if g == 0 else t0 - TL
                for blk in range(n // 128):
                    yb = [4, 5] if nxt("yb", 2) == 0 else [6, 7]
                    for c in range(8):
                        b = yb[c // 4]
                        T.op("pe", lambda E, c=c, b=b, blk=blk: E.transpose(psb[b][:, (c % 4) * 128:(c % 4 + 1) * 128], xT[:, c, blk * 128:(blk + 1) * 128], ident[:]),
                             [k_xT[c], k_const], [ps_tk[b]])
                    j = nxt("tmp", 2)
                    T.op("act", lambda E, j=j, b=yb[0]: E.copy(out=tmp[j][:, 0:512], in_=psb[b][:, :]), [ps_tk[yb[0]]], [k_tmp[j]])
                    T.op("dve", lambda E, j=j, b=yb[1]: E.tensor_copy(out=tmp[j][:, 512:1024], in_=psb[b][:, :]), [ps_tk[yb[1]]], [k_tmp[j]])
                    T.dma("pool", dst[r0 + blk * 128:r0 + (blk + 1) * 128, :], tmp[j][:, :], [k_tmp[j]], kd)

            def evac_store(pieces, func, dst2, kdst, t0, dt_bf=True, scale=None):
                j = nxt("stg", 3)
                n = 0
                for (ap_, tk_, c0, w) in pieces:
                    T.op("act", lambda E, ap_=ap_, c0=c0, w=w, j=j: E.activation(out=stg[j][0:ap_.shape[0], c0:c0 + w], in_=ap_, func=func), [tk_], [k_stg[j]])
                    n = c0 + w
                m = pieces[0][0].shape[0]
                T.dma("pool", dst2[:, t0:t0 + n], stg[j][0:m, 0:n], [k_stg[j]], kdst)

            def proj(l, t0, n, g):
                nsub = n // 512
                Wi = in_b[l]
                kWi = tk["in_%d" % l]
                kWr = tk["inr_%d" % l]
                rhs = lambda k, s_: hT[:, k, s_ * 512:(s_ + 1) * 512]
                for j_, src in enumerate([c_cosM, c_sinM]):
                    T.dma("sp", tab[j_][:, 0:n], src[:, t0:t0 + n], [], k_tab[j_])
                for u in range(6):
                    i = ring_load([(wsrc(Wi, 0, 8, u * 512, 512), 0, 8, 512)], [kWi])
                    v = slot_view(i, 0, 8, 512)
                    for c in range(4):
                        yb = [4, 5] if nxt("yb", 2) == 0 else [6, 7]
                        mm_group(yb, 128, lambda k: v[:, k, c * 128:(c + 1) * 128], rhs, 8, nsub, [ring_tk[i], k_hT])
                        evac_store(ps2(yb, 128, n), AF.Sigmoid, SG[l][u * 4 + c], TKn("SG%d" % l), t0)
                i = ring_load([(wsrc(Wi, 0, 8, O_QC, 416), 0, 8, 416)], [kWi])
                v = slot_view(i, 0, 8, 416)
                for c in range(2):
                    yb = [4, 5] if nxt("yb", 2) == 0 else [6, 7]
                    mm_group(yb, 128, lambda k: v[:, k, c * 128:(c + 1) * 128], rhs, 8, nsub, [ring_tk[i], k_hT])
                    for (ap_, tk_, c0, w) in ps2(yb, 128, n):
                        T.op("act", lambda E, ap_=ap_, c0=c0, w=w, c=c: E.copy(out=yT[:, c, c0:c0 + w], in_=ap_), [tk_], [k_yT[c]])
                rms_stats(lambda c: (yT[:, c, 0:n], k_yT[c]), 2, n, 256, rstd, k_rstd, sqt, k_sq, [])
                for c in range(2):
                    j = nxt("stg", 3)
                    T.op("dve", lambda E, c=c, j=j: E.scalar_tensor_tensor(out=stg[j][:, 0:n], in0=yT[:, c, 0:n], scalar=vt(l, V_QN + c, 1), in1=rstd[:, 0:n],
                                                                           op0=ALU.mult, op1=ALU.mult), [k_yT[c], k_rstd, k_VT], [k_stg[j]])
                    T.dma("pool", QCN[l][c][:, t0:t0 + n], stg[j][:, 0:n], [k_stg[j]], TKn("QCN%d" % l))
                yb = [4, 5] if nxt("yb", 2) == 0 else [6, 7]
                mm_group(yb, 128, lambda k: v[:, k, 256:384], rhs, 8, nsub, [ring_tk[i], k_hT])
                for (ap_, tk_, c0, w) in ps2(yb, 128, n):
                    T.op("act", lambda E, ap_=ap_, c0=c0, w=w: E.copy(out=yT[:, 2, c0:c0 + w], in_=ap_), [tk_], [k_yT[2]])
                rms_stats(lambda c: (yT[:, 2, 0:n], k_yT[2]), 1, n, 128, rstd, k_rstd, sqt, k_sq, [])
                j = nxt("stg", 3)
                T.op("dve", lambda E, j=j: E.scalar_tensor_tensor(out=stg[j][:, 0:n], in0=yT[:, 2, 0:n], scalar=vt(l, V_KVN, 1), in1=rstd[:, 0:n],
                                                                  op0=ALU.mult, op1=ALU.mult), [k_yT[2], k_rstd, k_VT], [k_stg[j]])
                T.dma("pool", CKVT[l][:, t0:t0 + n], stg[j][:, 0:n], [k_stg[j]], TKn("CKVT%d" % l))
                ir = ring_load([(wsrc(inr_b[l], 0, 8, 0, 32), 0, 8, 32)], [kWr])
                vr = slot_view(ir, 0, 8, 32)
                mm_group([4, 5], 32, lambda k: v[:, k, 384:416], rhs, 8, nsub, [ring_tk[i], k_hT])
                mm_group([6, 7], 32, lambda k: vr[:, k, 0:32], rhs, 8, nsub, [ring_tk[ir], k_hT])
                rope_out(32, [4, 5], [6, 7], tab[0], tab[1], k_tab[0], k_tab[1], n, KRT[l], TKn("KRT%d" % l), t0, 0)
                for j_, src in enumerate([c_cosR, c_sinR]):
                    T.dma("sp", tab[j_][:, 0:n], src[:, t0:t0 + n], [], k_tab[j_])
                for (ocol, rcol, dst, nm) in [(O_RQ, 32, RQT[l], "RQT%d" % l), (O_RK, 544, RKT[l], "RKT%d" % l)]:
                    i = ring_load([(wsrc(Wi, 0, 8, ocol, 512), 0, 8, 512)], [kWi])
                    v = slot_view(i, 0, 8, 512)
                    ir = ring_load([(wsrc(inr_b[l], 0, 8, rcol, 512), 0, 8, 512)], [kWr])
                    vr = slot_view(ir, 0, 8, 512)
                    for c in range(4):
                        mm_group([4, 5], 128, lambda k: v[:, k, c * 128:(c + 1) * 128], rhs, 8, nsub, [ring_tk[i], k_hT])
                        mm_group([6, 7], 128, lambda k: vr[:, k, c * 128:(c + 1) * 128], rhs, 8, nsub, [ring_tk[ir], k_hT])
                        rope_out(128, [4, 5], [6, 7], tab[0], tab[1], k_tab[0], k_tab[1], n, dst[c], TKn(nm), t0, 0)
                i = ring_load([(wsrc(Wi, 0, 8, O_RV, 512), 0, 8, 512)], [kWi])
                v = slot_view(i, 0, 8, 512)
                for blk in range(n // 128):
                    b = 4 + nxt("yb", 4)
                    for k in range(8):
                        T.op("pe", lambda E, k=k, b=b, blk=blk: E.matmul(psb[b][:, :], lhsT=hT[:, k, blk * 128:(blk + 1) * 128], rhs=v[:, k, :], start=(k == 0), stop=(k == 7)),
                             [ring_tk[i], k_hT[k]], [ps_tk[b]], inc=(k == 7))
                    j = nxt("stg", 3)
                    T.op("act", lambda E, j=j, b=b: E.copy(out=stg[j][:, 0:512], in_=psb[b][:, :]), [ps_tk[b]], [k_stg[j]])
                    T.dma("pool", RV[l][:, :, t0 // 128 + blk, :].rearrange("h p e -> p h e"), stg[j][:, 0:512].rearrange("p (h e) -> p h e", h=4), [k_stg[j]], TKn("RV%d" % l))
                i = ring_load([(wsrc(Wi, 0, 8, O_RG, 512), 0, 8, 512)], [kWi])
                v = slot_view(i, 0, 8, 512)
                for c in range(4):
                    yb = [4, 5] if nxt("yb", 2) == 0 else [6, 7]
                    mm_group(yb, 128, lambda k: v[:, k, c * 128:(c + 1) * 128], rhs, 8, nsub, [ring_tk[i], k_hT])
                    evac_store(ps2(yb, 128, n), AF.Silu, RGS[l][c], TKn("RGS%d" % l), t0)
                i = ring_load([(wsrc(Wi, 0, 8, O_PU, 512), 0, 8, 512)], [kWi])
                v = slot_view(i, 0, 8, 512)
                for c in range(4):
                    yb = [4, 5] if nxt("yb", 2) == 0 else [6, 7]
                    mm_group(yb, 128, lambda k: v[:, k, c * 128:(c + 1) * 128], rhs, 8, nsub, [ring_tk[i], k_hT])
                    j = nxt("tmp", 2)
                    for (ap_, tk_, c0, w) in ps2(yb, 128, n):
                        T.op("act", lambda E, ap_=ap_, c0=c0, w=w, j=j: E.copy(out=tmp[j][:, c0:c0 + w], in_=ap_), [tk_], [k_tmp[j]])
                    T.dma("pool", PUT[l][c][:, t0:t0 + n], tmp[j][:, 0:n], [k_tmp[j]], TKn("PUT%d" % l))
                if g == 1:
                    i = ring_load([(wsrc(Wi, 0, 8, O_KVC, 160), 0, 8, 160)], [kWi])
                    v = slot_view(i, 0, 8, 160)
                    for blk in range(n // 128):
                        b = 4 + nxt("yb", 4)
                        for k in range(8):
                            T.op("pe", lambda E, k=k, b=b, blk=blk: E.matmul(psb[b][:, 0:160], lhsT=hT[:, k, blk * 128:(blk + 1) * 128], rhs=v[:, k, :], start=(k == 0), stop=(k == 7)),
                                 [ring_tk[i], k_hT[k]], [ps_tk[b]], inc=(k == 7))
                        j = nxt("tmp", 2)
                        T.op("act", lambda E, j=j, b=b: E.copy(out=tmp[j][:, 0:160], in_=psb[b][:, 0:160]), [ps_tk[b]], [k_tmp[j]])
                        T.op("act", lambda E, j=j: E.activation(out=tmp[j][:, 256:384], in_=tmp[j][:, 0:128], func=AF.Square, accum_out=tmp[j][:, 512:513]),
                             [k_tmp[j]], [k_tmp[j]])
                        T.op("act", lambda E, j=j: E.activation(out=tmp[j][:, 512:513], in_=tmp[j][:, 512:513], func=AF.Sqrt, scale=1.0 / 128, bias=eps_t[:, 0:1]),
                             [k_tmp[j], k_const], [k_tmp[j]])
                        T.op("dve", lambda E, j=j: E.reciprocal(out=tmp[j][:, 512:513], in_=tmp[j][:, 512:513]), [k_tmp[j]], [k_tmp[j]])
                        T.op("dve", lambda E, j=j: E.scalar_tensor_tensor(out=tmp[j][:, 256:384], in0=tmp[j][:, 0:128], scalar=tmp[j][:, 512:513],
                                                                          in1=kvn_bc[:, l * 128:(l + 1) * 128], op0=ALU.mult, op1=ALU.mult), [k_tmp[j], k_const], [k_tmp[j]])
                        sq_, r_ = divmod(blk, 2)
                        T.dma("pool", ockv[sq_, l, r_ * 128:(r_ + 1) * 128, :], tmp[j][:, 256:384], [k_tmp[j]], k_out[2])
                        T.dma("pool", okr[sq_, l, r_ * 128:(r_ + 1) * 128, :], tmp[j][:, 128:160], [k_tmp[j]], k_out[3])

            def rope_out(M, ba, bb, tcos, tsin, kcos, ksin, n, dst2, kdst, t0, p0):
                j = nxt("stg", 3)
                for s_ in range(n // 512):
                    sl = slice(s_ * 512, (s_ + 1) * 512)
                    t1 = nxt("tmp", 2)
                    T.op("dve", lambda E, s_=s_, t1=t1, sl=sl: E.tensor_tensor(out=tmp[t1][p0:p0 + M, 0:512], in0=psb[ba[s_]][p0:p0 + M, :], in1=tcos[p0:p0 + M, sl], op=ALU.mult),
                         [ps_tk[ba[s_]], kcos], [k_tmp[t1]])
                    T.op("dve", lambda E, s_=s_, t1=t1, sl=sl: E.tensor_tensor(out=tmp[t1][p0:p0 + M, 512:1024], in0=psb[bb[s_]][p0:p0 + M, :], in1=tsin[p0:p0 + M, sl], op=ALU.mult),
                         [ps_tk[bb[s_]], ksin], [k_tmp[t1]])
                    T.op("pool", lambda E, t1=t1, sl=sl, j=j: E.tensor_tensor(out=stg[j][p0:p0 + M, sl], in0=tmp[t1][p0:p0 + M, 0:512], in1=tmp[t1][p0:p0 + M, 512:1024], op=ALU.add),
                         [k_tmp[t1]], [k_stg[j]])
                T.dma("pool", dst2[:, t0:t0 + n], stg[j][p0:p0 + M, 0:n], [k_stg[j]], kdst)

            def merge(l, t0, n, g):
                nsub = n // 512
                kin = [TKn("OA%d" % l), TKn("OB%d" % l), TKn("OC%d" % l)]
                T.dma("sp", aT[:, 0:4, 0:n], OA[l][:, t0:t0 + n].rearrange("(c p) t -> p c t", p=128), [kin[0]], k_aT)
                T.dma("sp", aT[:, 4:8, 0:n], OB[l][:, :, t0:t0 + n].rearrange("c p t -> p c t"), [kin[1]], k_aT)
                T.dma("sp", aT[:, 8:12, 0:n], OC[l][:, :, t0:t0 + n].rearrange("c p t -> p c t"), [kin[2]], k_aT)
                Wbs = [ba_b[l], br_b[l], bp_b[l]]
                kWs = [tk["ba_%d" % l], tk["br_%d" % l], tk["bp_%d" % l]]
                for u in range(4):
                    i = ring_load([(wsrc(Wbs[b_], 0, 4, u * 256, 256), b_ * 1024, 4, 256) for b_ in range(3)], kWs)
                    for jj in range(2):
                        dc = u * 2 + jj
                        for b_ in range(3):
                            T.dma("sp", gate[b_][:, 0:n], SG[l][b_ * 8 + dc][:, t0:t0 + n], [TKn("SG%d" % l)], k_gate[b_])
                        accj = None
                        for b_ in range(3):
                            v = slot_view(i, b_ * 1024, 4, 256)
                            yb = [4, 5] if nxt("yb", 2) == 0 else [6, 7]
                            mm_group(yb, 128, lambda k: v[:, k, jj * 128:(jj + 1) * 128], lambda k, s_: aT[:, b_ * 4 + k, s_ * 512:(s_ + 1) * 512], 4, nsub, [ring_tk[i], k_aT])
                            if b_ == 0:
                                accj = nxt("tmp", 2)
                                for (ap_, tk_, c0, w) in ps2(yb, 128, n):
                                    T.op("dve", lambda E, ap_=ap_, c0=c0, w=w: E.tensor_tensor(out=tmp[accj][:, c0:c0 + w], in0=ap_, in1=gate[0][:, c0:c0 + w], op=ALU.mult),
                                         [tk_, k_gate[0]], [k_tmp[accj]])
                            else:
                                o_ = 1 - accj
                                for (ap_, tk_, c0, w) in ps2(yb, 128, n):
                                    T.op("dve", lambda E, ap_=ap_, c0=c0, w=w, b_=b_: E.tensor_tensor(out=tmp[o_][:, c0:c0 + w], in0=ap_, in1=gate[b_][:, c0:c0 + w], op=ALU.mult),
                                         [tk_, k_gate[b_]], [k_tmp[o_]])
                                if b_ == 1:
                                    T.op("pool", lambda E: E.tensor_tensor(out=tmp[accj][:, 0:n], in0=tmp[accj][:, 0:n], in1=tmp[o_][:, 0:n], op=ALU.add),
                                         [k_tmp[o_], k_tmp[accj]], [k_tmp[accj]])
                                else:
                                    T.op("pool", lambda E, dc=dc: E.tensor_tensor(out=hT[:, dc, 0:n], in0=tmp[accj][:, 0:n], in1=tmp[o_][:, 0:n], op=ALU.add),
                                         [k_tmp[o_], k_tmp[accj]], [k_hT[dc]])
                load_x(t0, n, g)
                Wo = out_b[l]
                kWo = tk["out_%d" % l]
                for u in range(2):
                    i = ring_load([(wsrc(Wo, 0, 8, u * 512, 512), 0, 8, 512)], [kWo])
                    v = slot_view(i, 0, 8, 512)
                    for c in range(4):
                        dc = u * 4 + c
                        yb = [4, 5] if nxt("yb", 2) == 0 else [6, 7]
                        mm_group(yb, 128, lambda k: v[:, k, c * 128:(c + 1) * 128], lambda k, s_: hT[:, k, s_ * 512:(s_ + 1) * 512], 8, nsub, [ring_tk[i], k_hT])
                        for (ap_, tk_, c0, w) in ps2(yb, 128, n):
                            T.op("act", lambda E, ap_=ap_, c0=c0, w=w, dc=dc: E.copy(out=yT[:, dc, c0:c0 + w], in_=ap_), [tk_], [k_yT[dc]])
                post_resid(n, g, 1)

            gate = [sbx("gate%d" % j, [128, 1024], BF16) for j in range(3)]
            k_gate = [Tk("gate%d" % j) for j in range(3)]

            for (t0, n, g) in STS:
                if phase == 1:
                    load_x(t0, n, g)
                    ffn(n, g, 0, 0)
                    store_fm(X1T[l], TKn("X1T%d" % l), t0, n)
                    prenorm(n, g, 1)
                    proj(l, t0, n, g)
                else:
                    merge(l, t0, n, g)
                    ffn(n, g, 2, 1)
                    if l == L - 1:
                        store_final(t0, n, g)
                    else:
                        store_fm(XT1, TKn("XT1"), t0, n)
            T.barrier(list(tk.values()) + ring_tk + k_out)
            T.end_phase()
            T.end_phase()

        SCALE = 96.0 ** -0.5

        def run_attention(l, esx):
            def sbx(name, shape, dt):
                return esx.enter_context(nc.sbuf_tensor(name + "_a%d" % l, list(shape), dt))
            NKMAX = TL + PAST
            ckvT = sbx("ckvT", [128, NKMAX], BF16)
            KhT = [sbx("KhT%d" % j, [96, NKMAX], BF16) for j in range(2)]
            Vall = sbx("Vall", [128, 36 * 8 * 65], BF16)
            wukv = sbx("wukv", [128, 1024], BF16)
            wuq = sbx("wuq", [128, 2 * 768], BF16)
            wuqr = sbx("wuqr", [128, 2 * 768], BF16)
            qcn = [sbx("qcn%d" % j, [128, 2 * 512], BF16) for j in range(2)]
            QhT = [sbx("QhT%d" % j, [96, 512], BF16) for j in range(2)]
            PT = [sbx("PT%d" % j, [128, 512], BF16) for j in range(3)]
            tabc = [sbx("tabc%d" % j, [96, 512], F32) for j in range(2)]
            tabs = [sbx("tabs%d" % j, [96, 512], F32) for j in range(2)]
            Of = [sbx("Of%d" % j, [65, 512], F32) for j in range(2)]
            rec = [sbx("rec%d" % j, [65, 512], F32) for j in range(2)]
            oab = [sbx("oab%d" % j, [64, 512], BF16) for j in range(2)]
            rt1 = [sbx("rt1%d" % j, [96, 1024], F32) for j in range(2)]
            cst = sbx("cst", [128, 4 * 128], F32)
            cst2 = sbx("cst2", [128, 4 * 96], F32)
            k_ckvT, k_V, k_w, k_cst = Tk("ckvT"), Tk("V"), Tk("w"), Tk("cst")
            k_Kh = [Tk("Kh0"), Tk("Kh1")]
            k_qcn = [Tk("qcn0"), Tk("qcn1")]
            k_Qh = [Tk("Qh0"), Tk("Qh1")]
            k_PT = [Tk("PT%d" % j) for j in range(3)]
            k_tab = [Tk("tb0"), Tk("tb1")]
            k_Of = [Tk("Of0"), Tk("Of1")]
            k_rec = [Tk("rec0"), Tk("rec1")]
            k_oab = [Tk("oab0"), Tk("oab1")]
            k_rt1 = [Tk("rt10"), Tk("rt11")]
            T.dma("sp", wukv[:], ukv_b[l][:, :], [tk["ukv_%d" % l]], k_w)
            T.dma("sp", wuq[:].rearrange("p (k n) -> p k n", k=2), uq_b[l].rearrange("(k p) n -> p k n", p=128), [tk["uq_%d" % l]], k_w)
            T.dma("sp", wuqr[:].rearrange("p (k n) -> p k n", k=2), uqr_b[l].rearrange("(k p) n -> p k n", p=128), [tk["uqr_%d" % l]], k_w)
            wuq3 = wuq[:].rearrange("p (k n) -> p k n", k=2)
            wuqr3 = wuqr[:].rearrange("p (k n) -> p k n", k=2)
            V4 = Vall[:].rearrange("p (t h e) -> p t h e", h=8, e=65)
            T.op("dve", lambda E: E.memset(Vall[:], 1.0), [], [k_V])
            qi = [0]
            seqs = [(0, TL, True), (TL, 256, False), (TL + 256, 256, False)]
            for (s0, ns, has_cache) in seqs:
                nk = ns + (PAST if has_cache else 0)
                nkt = nk // 128
                T.dma("sp", ckvT[:, 0:ns], CKVT[l][:, s0:s0 + ns], [TKn("CKVT%d" % l)], k_ckvT)
                for j in range(2):
                    T.dma("sp", KhT[j][64:96, 0:ns], KRT[l][:, s0:s0 + ns], [TKn("KRT%d" % l)], k_Kh[j])
                if has_cache:
                    T.dma("sp", cst[:].rearrange("p (b f) -> p b f", b=4), cckv[l].rearrange("(b p) f -> p b f", p=128), [], k_cst)
                    T.op("dve", lambda E: E.memset(cst2[:], 0.0), [], [k_cst])
                    T.dma("sp", cst2[:].rearrange("p (b f) -> p b f", b=4)[:, :, 64:96], ckr[l].rearrange("(b p) f -> p b f", p=128), [], k_cst)
                    for b_ in range(4):
                        T.op("pe", lambda E, b_=b_: E.transpose(psb[0][:, b_ * 128:(b_ + 1) * 128], cst[:, b_ * 128:(b_ + 1) * 128], ident[:]), [k_cst, k_const], [ps_tk[0]])
                        T.op("pe", lambda E, b_=b_: E.transpose(psb[1][0:96, b_ * 128:(b_ + 1) * 128], cst2[:, b_ * 96:(b_ + 1) * 96], ident[:]), [k_cst, k_const], [ps_tk[1]])
                    T.op("act", lambda E: E.copy(out=ckvT[:, ns:ns + 512], in_=psb[0][:, :]), [ps_tk[0]], [k_ckvT])
                    for j in range(2):
                        T.op("act", lambda E, j=j: E.copy(out=KhT[j][64:96, ns:ns + 512], in_=psb[1][64:96, :]), [ps_tk[1]], [k_Kh[j]])
                wv = wukv[:].rearrange("p (h c) -> p h c", h=8)[:, :, 64:128]
                for kt in range(nkt):
                    b = kt % 2
                    T.op("pe", lambda E, kt=kt, b=b: E.matmul(psb[b][:, :].rearrange("p (h e) -> p h e", h=8), lhsT=ckvT[:, kt * 128:(kt + 1) * 128], rhs=wv, start=True, stop=True),
                         [k_ckvT, k_w], [ps_tk[b]])
                    T.op("act" if kt % 2 == 0 else "dve", lambda E, kt=kt, b=b: (E.copy if kt % 2 == 0 else E.tensor_copy)(
                        out=V4[:, kt, :, 0:64], in_=psb[b][:, :].rearrange("p (h e) -> p h e", h=8)), [ps_tk[b]], [k_V])
                nq = ns
                qn = min(512, nq)

                def kproj(h):
                    kj = h % 2
                    for blk in range((nk + 511) // 512):
                        w = min(512, nk - blk * 512)
                        b = blk % 2
                        T.op("pe", lambda E, blk=blk, w=w, b=b, h=h: E.matmul(psb[b][0:64, 0:w], lhsT=wukv[:, h * 128:h * 128 + 64], rhs=ckvT[:, blk * 512:blk * 512 + w], start=True, stop=True),
                             [k_w, k_ckvT], [ps_tk[b]])
                        T.op("act" if blk % 2 == 0 else "dve", lambda E, blk=blk, w=w, b=b: (E.copy if blk % 2 == 0 else E.tensor_copy)(
                            out=KhT[kj][0:64, blk * 512:blk * 512 + w], in_=psb[b][0:64, 0:w]), [ps_tk[b]], [k_Kh[kj]])

                def prologue(h, qt):
                    t0 = s0 + qt * qn
                    j = qi[0] % 2
                    qi[0] += 1
                    T.dma("sp", qcn[j][:, :].rearrange("p (k n) -> p k n", k=2)[:, :, 0:qn], QCN[l][:, :, t0:t0 + qn].rearrange("k p t -> p k t"), [TKn("QCN%d" % l)], k_qcn[j])
                    T.dma("sp", tabc[j][64:96, 0:qn], c_cosM[64:96, t0:t0 + qn], [], k_tab[j])
                    T.dma("sp", tabs[j][64:96, 0:qn], c_sinM[64:96, t0:t0 + qn], [], k_tab[j])
                    q3 = qcn[j][:, :].rearrange("p (k n) -> p k n", k=2)
                    for k in range(2):
                        T.op("pe", lambda E, k=k: E.matmul(psb[0][0:96, 0:qn], lhsT=wuq3[:, k, h * 96:(h + 1) * 96], rhs=q3[:, k, 0:qn], start=(k == 0), stop=(k == 1)),
                             [k_w, k_qcn[j]], [ps_tk[0]], inc=(k == 1))
                    for k in range(2):
                        T.op("pe", lambda E, k=k: E.matmul(psb[1][0:96, 0:qn], lhsT=wuqr3[:, k, h * 96:(h + 1) * 96], rhs=q3[:, k, 0:qn], start=(k == 0), stop=(k == 1)),
                             [k_w, k_qcn[j]], [ps_tk[1]], inc=(k == 1))
                    T.op("dve", lambda E: E.tensor_copy(out=QhT[j][0:64, 0:qn], in_=psb[0][0:64, 0:qn]), [ps_tk[0]], [k_Qh[j]])
                    T.op("dve", lambda E: E.tensor_tensor(out=rt1[j][64:96, 0:qn], in0=psb[0][64:96, 0:qn], in1=tabc[j][64:96, 0:qn], op=ALU.mult), [ps_tk[0], k_tab[j]], [k_rt1[j]])
                    T.op("dve", lambda E: E.tensor_tensor(out=rt1[j][64:96, 512:512 + qn], in0=psb[1][64:96, 0:qn], in1=tabs[j][64:96, 0:qn], op=ALU.mult), [ps_tk[1], k_tab[j]], [k_rt1[j]])
                    T.op("pool", lambda E: E.tensor_tensor(out=QhT[j][64:96, 0:qn], in0=rt1[j][64:96, 0:qn], in1=rt1[j][64:96, 512:512 + qn], op=ALU.add), [k_rt1[j]], [k_Qh[j]])
                    return j

                def mainloop(h, qt, j, pending):
                    t0 = s0 + qt * qn
                    kj = h % 2
                    ob = 6 + j
                    SB = [2, 3, 4]
                    for it in range(nkt + 2):
                        if it < nkt:
                            sbk = SB[it % 3]
                            T.op("pe", lambda E, it=it, sbk=sbk: E.matmul(psb[sbk][:, 0:qn], lhsT=KhT[kj][:, it * 128:(it + 1) * 128], rhs=QhT[j][:, 0:qn], start=True, stop=True),
                                 [k_Kh[kj], k_Qh[j]], [ps_tk[sbk]])
                        if 1 <= it <= nkt:
                            i1 = it - 1
                            sbk = SB[i1 % 3]
                            T.op("act", lambda E, i1=i1, sbk=sbk: E.activation(out=PT[i1 % 3][:, 0:qn], in_=psb[sbk][:, 0:qn], func=AF.Exp, scale=SCALE), [ps_tk[sbk]], [k_PT[i1 % 3]])
                        if it >= 2:
                            i2 = it - 2
                            T.op("pe", lambda E, i2=i2: E.matmul(psb[ob][0:65, 0:qn], lhsT=V4[:, i2, h, :], rhs=PT[i2 % 3][:, 0:qn], start=(i2 == 0), stop=(i2 == nkt - 1)),
                                 [k_V, k_PT[i2 % 3]], [ps_tk[ob]], inc=(i2 == nkt - 1))
                        if pending is not None and it == min(10, nkt):
                            pending()
                            pending = None
                    if pending is not None:
                        pending()
                    T.op("act", lambda E: E.copy(out=Of[j][:, 0:qn], in_=psb[ob][0:65, 0:qn]), [ps_tk[ob]], [k_Of[j]])
                    T.op("dve", lambda E: E.reciprocal(out=rec[j][64:65, 0:qn], in_=Of[j][64:65, 0:qn]), [k_Of[j]], [k_rec[j]])

                    def ep2():
                        T.op("pe", lambda E: E.matmul(psb[5][0:64, 0:qn], lhsT=ones_f[64:65, 0:64], rhs=rec[j][64:65, 0:qn], start=True, stop=True), [k_rec[j], k_const], [ps_tk[5]])
                        T.op("dve", lambda E: E.tensor_tensor(out=oab[j][:, 0:qn], in0=Of[j][0:64, 0:qn], in1=psb[5][0:64, 0:qn], op=ALU.mult), [k_Of[j], ps_tk[5]], [k_oab[j]])
                        T.dma("pool", OA[l][h * 64:(h + 1) * 64, t0:t0 + qn], oab[j][:, 0:qn], [k_oab[j]], TKn("OA%d" % l))
                    return ep2

                units = [(h, qt) for h in range(8) for qt in range(nq // qn)]
                kproj(0)
                jcur = prologue(*units[0])
                pend = None
                for idx, (h, qt) in enumerate(units):
                    jn = None
                    if idx + 1 < len(units):
                        h2, qt2 = units[idx + 1]
                        if h2 != h:
                            kproj(h2)
                        jn = prologue(h2, qt2)
                    pend = mainloop(h, qt, jcur, pend)
                    jcur = jn
                pend()
            T.barrier(list(tk.values()) + ring_tk + k_out)
            T.end_phase()

        def run_retention(l, esx):
            def sbx(name, shape, dt):
                return esx.enter_context(nc.sbuf_tensor(name + "_r%d" % l, list(shape), dt))
            kT = sbx("kT", [128, TL], BF16)
            qT = sbx("qT", [128, TL], BF16)
            vtm = sbx("vtm", [128, 32 * 128], BF16)
            kdf = sbx("kdf", [128, 32 * 128], BF16)
            kdb = sbx("kdb", [128, 32 * 128], BF16)
            Sf = sbx("Sf", [128, 33 * 128], BF16)
            Sb = sbx("Sb", [128, 33 * 128], BF16)
            S = [sbx("S%d" % j, [128, 128], F32) for j in range(2)]
            qdf = [sbx("qdf%d" % j, [128, 512], BF16) for j in range(2)]
            qdb = [sbx("qdb%d" % j, [128, 512], BF16) for j in range(2)]
            sd = [sbx("sd%d" % j, [128, 512], BF16) for j in range(2)]
            sq = [sbx("sqr%d" % j, [128, 512], BF16) for j in range(2)]
            rs = [sbx("rsr%d" % j, [128, 512], F32) for j in range(2)]
            tt = [sbx("ttr%d" % j, [128, 512], F32) for j in range(2)]
            rg = [sbx("rgr%d" % j, [128, 512], BF16) for j in range(2)]
            obt = [sbx("obr%d" % j, [128, 512], BF16) for j in range(2)]
            k_kT, k_qT, k_v, k_kdf, k_kdb, k_Sf, k_Sb = Tk("kT"), Tk("qT"), Tk("v"), Tk("kdf"), Tk("kdb"), Tk("Sf"), Tk("Sb")
            k_S = [Tk("S0"), Tk("S1")]
            k2 = {nm: [Tk(nm + "0"), Tk(nm + "1")] for nm in ["qdf", "qdb", "sd", "sq", "rs", "tt", "rg", "ob"]}
            RT = sbx("RT", [128, 4 * 1536], F32)
            RS = sbx("RS", [128, 4 * 4], F32)
            cret = sbx("cret", [128, 6 * 512], F32)
            ckj = sbx("ckj", [128, 2], F32)
            lg = sbx("lg", [128, L * 8], F32)
            tmpa = sbx("tmpa", [128, 512], F32)
            tmpb = sbx("tmpb", [128, 512], F32)
            k_cret, k_lg, k_ta, k_tb, k_RT, k_RS = Tk("cret"), Tk("lg"), Tk("ta"), Tk("tb"), Tk("RT"), Tk("RS")
            for i in range(6):
                T.dma("sp", cret[:, i * 512:(i + 1) * 512], c_ret[i], [], k_cret)
            T.dma("sp", ckj[:], c_kj[:, :], [], k_cret)
            T.dma("sp", lg[:], ret_decay.rearrange("l e -> (l e)").partition_broadcast(128), [], k_lg)
            T.op("act", lambda E: E.activation(out=lg[:], in_=lg[:], func=AF.Exp), [k_lg], [k_lg])
            T.op("dve", lambda E: E.tensor_scalar(out=lg[:], in0=lg[:], scalar1=-1.0, scalar2=None, op0=ALU.mult), [k_lg], [k_lg])
            DKS = 128.0 ** -0.5
            for h in range(4):
                lf = lg[:, l * 8 + h:l * 8 + h + 1]
                lb = lg[:, l * 8 + 4 + h:l * 8 + 4 + h + 1]
                o = h * 1536
                T.op("act", lambda E: E.activation(out=tmpa[:], in_=cret[:, 0:512], func=AF.Exp, scale=lf), [k_cret, k_lg], [k_ta])
                T.op("dve", lambda E: E.tensor_tensor(out=tmpa[:], in0=tmpa[:], in1=cret[:, 1024:1536], op=ALU.mult), [k_ta, k_cret], [k_ta])
                T.op("act", lambda E: E.activation(out=tmpb[:], in_=cret[:, 512:1024], func=AF.Exp, scale=lb), [k_cret, k_lg], [k_tb])
                T.op("dve", lambda E: E.tensor_tensor(out=tmpb[:], in0=tmpb[:], in1=cret[:, 1536:2048], op=ALU.mult), [k_tb, k_cret], [k_tb])
                T.op("dve", lambda E: E.tensor_tensor(out=RT[:, o:o + 512], in0=tmpa[:], in1=tmpb[:], op=ALU.add), [k_ta, k_tb], [k_RT])
                T.op("dve", lambda E: E.tensor_scalar(out=RT[:, o:o + 512], in0=RT[:, o:o + 512], scalar1=DKS, scalar2=None, op0=ALU.mult), [k_RT], [k_RT])
                T.op("act", lambda E: E.activation(out=RT[:, o + 512:o + 1024], in_=cret[:, 2048:2560], func=AF.Exp, scale=lf), [k_cret, k_lg], [k_RT])
                T.op("act", lambda E: E.activation(out=RT[:, o + 1024:o + 1536], in_=cret[:, 2560:3072], func=AF.Exp, scale=lb), [k_cret, k_lg], [k_RT])
                r = h * 4
                T.op("act", lambda E: E.activation(out=RS[:, r:r + 1], in_=ckj[:, 0:1], func=AF.Exp, scale=lf), [k_cret, k_lg], [k_RS])
                T.op("act", lambda E: E.activation(out=RS[:, r + 1:r + 2], in_=ckj[:, 1:2], func=AF.Exp, scale=lb), [k_cret, k_lg], [k_RS])
                T.op("dve", lambda E: E.tensor_scalar(out=RS[:, r:r + 2], in0=RS[:, r:r + 2], scalar1=DKS, scalar2=None, op0=ALU.mult), [k_RS], [k_RS])
                T.op("act", lambda E: E.activation(out=RS[:, r + 2:r + 3], in_=lf, func=AF.Exp, scale=128.0), [k_lg], [k_RS])
                T.op("act", lambda E: E.activation(out=RS[:, r + 3:r + 4], in_=lb, func=AF.Exp, scale=128.0), [k_lg], [k_RS])
            psT = psb[7][:, :].bitcast(BF16)
            if RET_STOP <= 1:
                T.barrier(); return
            gi = [0]
            seqs = [(0, TL, 0, None), (TL, 256, 1, 0), (TL + 256, 256, 1, 1)]
            for (s0, ns, is_p, pidx) in seqs:
                NC_ = ns // 128
                for h in range(4):
                    r = h * 4
                    o = h * 1536
                    T.dma("sp", kT[:, 0:ns], RKT[l][h][:, s0:s0 + ns], [TKn("RKT%d" % l)], k_kT)
                    T.dma("sp", qT[:, 0:ns], RQT[l][h][:, s0:s0 + ns], [TKn("RQT%d" % l)], k_qT)
                    T.dma("sp", vtm[:, 0:NC_ * 128].rearrange("p (c e) -> p c e", e=128), RV[l][h][:, s0 // 128:s0 // 128 + NC_, :], [TKn("RV%d" % l)], k_v)
                    if RET_STOP <= 1.5:
                        T.barrier(); return
                    for c0 in range(0, NC_, 8):
                        nb = min(8, NC_ - c0)
                        for c in range(c0, c0 + nb):
                            T.op("pe", lambda E, c=c, c0=c0: E.transpose(psT[:, (c - c0) * 128:(c - c0 + 1) * 128], kT[:, c * 128:(c + 1) * 128], ones_b[:] if False else identb[:]),
                                 [k_kT, k_const], [ps_tk[7]])
                        T.op("act", lambda E, c0=c0, nb=nb: E.activation(out=kdf[:, c0 * 128:(c0 + nb) * 128], in_=psT[:, 0:nb * 128], func=AF.Copy, scale=RS[:, r:r + 1]),
                             [ps_tk[7], k_RS], [k_kdf])
                        T.op("dve", lambda E, c0=c0, nb=nb: E.tensor_scalar(out=kdb[:, c0 * 128:(c0 + nb) * 128], in0=psT[:, 0:nb * 128], scalar1=RS[:, r + 1:r + 2], scalar2=None, op0=ALU.mult),
                             [ps_tk[7], k_RS, k_kdf], [k_kdb])
                    if RET_STOP <= 2:
                        T.barrier(); return
                    for dr in range(2):
                        if is_p:
                            T.op("dve", lambda E, dr=dr: E.memset(S[dr][:], 0.0), [], [k_S[dr]])
                        else:
                            T.dma("sp", S[dr][:], st0[l, dr, h], [], k_S[dr])
                    for ii in range(NC_):
                        for dr in range(2):
                            Sx, kSx, kd, kkd = (Sf, k_Sf, kdf, k_kdf) if dr == 0 else (Sb, k_Sb, kdb, k_kdb)
                            c = ii if dr == 0 else NC_ - 1 - ii
                            cd = RS[:, r + 2 + dr:r + 3 + dr]
                            T.op("act", lambda E, c=c, dr=dr, Sx=Sx: E.copy(out=Sx[:, c * 128:(c + 1) * 128], in_=S[dr][:]), [k_S[dr]], [kSx])
                            b = 2 * dr + (ii % 2)
                            T.op("pe", lambda E, c=c, b=b, kd=kd: E.matmul(psb[b][:, 0:128], lhsT=kd[:, c * 128:(c + 1) * 128], rhs=vtm[:, c * 128:(c + 1) * 128], start=True, stop=True),
                                 [kkd, k_v], [ps_tk[b]])
                            T.op("dve", lambda E, b=b, dr=dr, cd=cd: E.scalar_tensor_tensor(out=S[dr][:], in0=S[dr][:], scalar=cd, in1=psb[b][:, 0:128], op0=ALU.mult, op1=ALU.add),
                                 [k_S[dr], ps_tk[b], k_RS], [k_S[dr]])
                    if is_p:
                        for dr in range(2):
                            T.dma("pool", ost[pidx, l, dr, h], S[dr][:], [k_S[dr]], k_out[4])
                    if RET_STOP <= 3:
                        T.barrier(); return
                    for g0 in range(0, NC_, 4):
                        ng = min(4, NC_ - g0)
                        w = ng * 128
                        j = gi[0] % 2
                        gi[0] += 1
                        tok = slice(g0 * 128, g0 * 128 + w)
                        T.op("dve", lambda E, j=j: E.tensor_tensor(out=qdf[j][:, 0:w], in0=qT[:, tok], in1=RT[:, o + 512:o + 512 + w], op=ALU.mult), [k_qT, k_RT], [k2["qdf"][j]])
                        T.op("pool", lambda E, j=j: E.tensor_tensor(out=qdb[j][:, 0:w], in0=qT[:, tok], in1=RT[:, o + 1024:o + 1024 + w], op=ALU.mult), [k_qT, k_RT], [k2["qdb"][j]])
                        sbk = 2 + j
                        for ci in range(ng):
                            c = g0 + ci
                            T.op("pe", lambda E, c=c, ci=ci: E.matmul(psb[sbk][:, ci * 128:(ci + 1) * 128], lhsT=kT[:, c * 128:(c + 1) * 128], rhs=qT[:, c * 128:(c + 1) * 128], start=True, stop=True),
                                 [k_kT, k_qT], [ps_tk[sbk]])
                        T.op("dve", lambda E, j=j: E.tensor_tensor(out=sd[j][:, 0:w], in0=psb[sbk][:, 0:w], in1=RT[:, o:o + w], op=ALU.mult), [ps_tk[sbk], k_RT], [k2["sd"][j]])
                        obk = 4 + j
                        for ci in range(ng):
                            c = g0 + ci
                            cs = slice(ci * 128, (ci + 1) * 128)
                            T.op("pe", lambda E, c=c, cs=cs: E.matmul(psb[obk][:, cs], lhsT=vtm[:, c * 128:(c + 1) * 128], rhs=sd[j][:, cs], start=True, stop=False), [k_v, k2["sd"][j]], [ps_tk[obk]], inc=False)
                            T.op("pe", lambda E, c=c, cs=cs: E.matmul(psb[obk][:, cs], lhsT=Sf[:, c * 128:(c + 1) * 128], rhs=qdf[j][:, cs], start=False, stop=False), [k_Sf, k2["qdf"][j]], [ps_tk[obk]], inc=False)
                            T.op("pe", lambda E, c=c, cs=cs: E.matmul(psb[obk][:, cs], lhsT=Sb[:, c * 128:(c + 1) * 128], rhs=qdb[j][:, cs], start=False, stop=True), [k_Sb, k2["qdb"][j]], [ps_tk[obk]], inc=True)
                        T.op("act", lambda E, j=j: E.activation(out=sq[j][:, 0:w], in_=psb[obk][:, 0:w], func=AF.Square), [ps_tk[obk]], [k2["sq"][j]])
                        T.op("pe", lambda E, j=j: E.matmul(psb[6][:, 0:w], lhsT=ones_b[:], rhs=sq[j][:, 0:w], start=True, stop=True), [k2["sq"][j], k_const], [ps_tk[6]])
                        T.op("act", lambda E, j=j: E.activation(out=rs[j][:, 0:w], in_=psb[6][:, 0:w], func=AF.Sqrt, scale=1.0 / 128, bias=eps_t[:, 0:1]), [ps_tk[6], k_const], [k2["rs"][j]])
                        T.op("dve", lambda E, j=j: E.reciprocal(out=rs[j][:, 0:w], in_=rs[j][:, 0:w]), [k2["rs"][j]], [k2["rs"][j]])
                        T.op("dve", lambda E, j=j: E.tensor_tensor(out=tt[j][:, 0:w], in0=psb[obk][:, 0:w], in1=rs[j][:, 0:w], op=ALU.mult), [ps_tk[obk], k2["rs"][j]], [k2["tt"][j]])
                        T.dma("sp", rg[j][:, 0:w], RGS[l][h][:, s0 + g0 * 128:s0 + g0 * 128 + w], [TKn("RGS%d" % l)], k2["rg"][j])
                        T.op("pool", lambda E, j=j: E.tensor_tensor(out=obt[j][:, 0:w], in0=tt[j][:, 0:w], in1=rg[j][:, 0:w], op=ALU.mult), [k2["tt"][j], k2["rg"][j]], [k2["ob"][j]])
                        T.dma("pool", OB[l][h][:, s0 + g0 * 128:s0 + g0 * 128 + w], obt[j][:, 0:w], [k2["ob"][j]], TKn("OB%d" % l))
            T.barrier(list(tk.values()) + ring_tk + k_out)
            T.end_phase()

        identb = sb("identb", [128, 128], BF16)
        T.op("dve", lambda E: E.tensor_copy(out=identb[:], in_=ident[:]), [k_const], [k_const])

        def run_pool(l, esx):
            def sbx(name, shape, dt):
                return esx.enter_context(nc.sbuf_tensor(name + "_p%d" % l, list(shape), dt))
            W_ = 1024 + 16
            U = [sbx("U%d" % j, [128, W_], F32) for j in range(2)]
            A2 = [sbx("A2%d" % j, [128, W_], F32) for j in range(2)]
            A4 = [sbx("A4%d" % j, [128, W_], F32) for j in range(2)]
            rc = [sbx("rc%d" % j, [128, 1024], F32) for j in range(2)]
            yb_ = [sbx("ypb%d" % j, [128, 1024], BF16) for j in range(2)]
            oc = [sbx("ocb%d" % j, [128, 1024], BF16) for j in range(2)]
            pw = sbx("pw", [128, 4 * 128], BF16)
            kk = {nm: [Tk(nm + "0"), Tk(nm + "1")] for nm in ["U", "A2", "A4", "rc", "y", "oc"]}
            k_pw = Tk("pw")
            T.dma("sp", pw[:].rearrange("p (g d) -> p g d", g=4), pw_b[l].rearrange("(g p) d -> p g d", p=128), [tk["pw_%d" % l]], k_pw)
            it = [0]
            for (s0, ns) in [(0, TL), (TL, 256), (TL + 256, 256)]:
                for a0 in range(0, ns, 1024):
                    n = min(1024, ns - a0)
                    for g in range(4):
                        j = it[0] % 2
                        it[0] += 1
                        lo = max(a0 - 8, 0)
                        hi = min(a0 + n + 8, ns)
                        T.op("pool", lambda E, j=j: E.memset(U[j][:, :], 0.0), [], [kk["U"][j]])
                        T.dma("sp", U[j][:, 8 + lo - a0:8 + hi - a0], PUT[l][g][:, s0 + lo:s0 + hi], [TKn("PUT%d" % l)], kk["U"][j])
                        T.dma("sp", rc[j][:, 0:n], c_rcnt[g:g + 1, s0 + a0:s0 + a0 + n].partition_broadcast(128), [], kk["rc"][j])
                        m = n + 16
                        T.op("dve", lambda E, j=j: E.tensor_tensor(out=A2[j][:, 1:m], in0=U[j][:, 0:m - 1], in1=U[j][:, 1:m], op=ALU.add), [kk["U"][j]], [kk["A2"][j]])
                        cur, kcur = A2[j], kk["A2"][j]
                        if g >= 1:
                            T.op("dve", lambda E, j=j: E.tensor_tensor(out=A4[j][:, 2:m - 1], in0=A2[j][:, 1:m - 2], in1=A2[j][:, 3:m], op=ALU.add), [kk["A2"][j]], [kk["A4"][j]])
                            cur, kcur = A4[j], kk["A4"][j]
                        if g >= 2:
                            T.op("dve", lambda E, j=j: E.tensor_tensor(out=A2[j][:, 4:m - 3], in0=A4[j][:, 2:m - 5], in1=A4[j][:, 6:m - 1], op=ALU.add), [kk["A4"][j], kk["A2"][j]], [kk["A2"][j]])
                            cur, kcur = A2[j], kk["A2"][j]
                        if g >= 3:
                            T.op("dve", lambda E, j=j: E.tensor_tensor(out=A4[j][:, 8:m - 7], in0=A2[j][:, 4:m - 11], in1=A2[j][:, 12:m - 3], op=ALU.add), [kk["A2"][j], kk["A4"][j]], [kk["A4"][j]])
                            cur, kcur = A4[j], kk["A4"][j]
                        T.op("dve", lambda E, j=j, cur=cur: E.tensor_tensor(out=rc[j][:, 0:n], in0=cur[:, 8:8 + n], in1=rc[j][:, 0:n], op=ALU.mult), [kcur, kk["rc"][j]], [kk["rc"][j]])
                        T.op("dve", lambda E, j=j: E.tensor_tensor(out=yb_[j][:, 0:n], in0=rc[j][:, 0:n], in1=U[j][:, 8:8 + n], op=ALU.subtract), [kk["rc"][j], kk["U"][j]], [kk["y"][j]])
                        for s_ in range((n + 511) // 512):
                            w = min(512, n - s_ * 512)
                            b = 2 * j + s_
                            T.op("pe", lambda E, s_=s_, w=w, b=b, g=g, j=j: E.matmul(psb[b][:, 0:w], lhsT=pw[:, g * 128:(g + 1) * 128], rhs=yb_[j][:, s_ * 512:s_ * 512 + w], start=True, stop=True),
                                 [k_pw, kk["y"][j]], [ps_tk[b]])
                            T.op("act", lambda E, s_=s_, w=w, b=b, g=g, j=j: E.activation(out=oc[j][:, s_ * 512:s_ * 512 + w], in_=psb[b][:, 0:w], func=AF.Copy, scale=vt(l, V_PS + g, 1)),
                                 [ps_tk[b], k_VT], [kk["oc"][j]])
                        T.dma("pool", OC[l][g][:, s0 + a0:s0 + a0 + n], oc[j][:, 0:n], [kk["oc"][j]], TKn("OC%d" % l))
            T.barrier(list(tk.values()) + ring_tk + k_out)
            T.end_phase()

        stop = debug_stop
        done = False
        for l in range(L):
            if stop == "setup":
                break
            with ExitStack() as esx:
                run_rowlocal(l, 1, esx)
            if stop == ("p1", l):
                break
            with ExitStack() as esx:
                run_attention(l, esx)
            if stop == ("att", l):
                break
            with ExitStack() as esx:
                run_retention(l, esx)
            if stop == ("ret", l):
                break
            with ExitStack() as esx:
                run_pool(l, esx)
            if stop == ("p2", l):
                break
            with ExitStack() as esx:
                run_rowlocal(l, 3, esx)
            if stop == ("p3", l):
                break
        if stop is not None and stop != "setup":
            if stop == ("p3", 0):
                dd = dout("dbg_XT1", [8, 128, TT])
                kd2 = Tk("dbg2")
                for i in range(8):
                    T.dma("pool", dd[i], XT1[i], [TKn("XT1")], kd2)
            kd = Tk("dbg")
            for nm, src in [("OA", OA[0]), ("OB", OB[0].rearrange("c p t -> (c p) t")), ("OC", OC[0].rearrange("c p t -> (c p) t"))]:
                dd = dout("dbg_" + nm, [512, TT])
                for i in range(4):
                    T.dma("pool", dd[i * 128:(i + 1) * 128, :], src[i * 128:(i + 1) * 128, :], [TKn("OA0"), TKn("OB0"), TKn("OC0")], kd)
        T.barrier(list(tk.values()) + ring_tk + k_out)
        build_program.ninst = T.ninst
    return nc


def _tables():
    inv8 = 10000.0 ** (-np.arange(8, dtype=np.float32) / 8)
    inv64 = 10000.0 ** (-np.arange(64, dtype=np.float32) / 64)
    t = np.arange(TL)
    row = (t // 64).astype(np.float32)
    col = (t % 64).astype(np.float32)
    cosM = np.ones((128, TT), np.float32)
    sinM = np.zeros((128, TT), np.float32)
    for p in range(128):
        r = p % 32
        pos = row if r < 16 else col
        f = inv8[r % 8]
        sgn = -1.0 if (r % 16) < 8 else 1.0
        ang = (pos * f).astype(np.float32)
        cosM[p, :TL] = np.cos(ang)
        sinM[p, :TL] = sgn * np.sin(ang)
    cosR = np.ones((128, TT), np.float32)
    sinR = np.zeros((128, TT), np.float32)
    for p in range(128):
        f = inv64[p % 64]
        sgn = -1.0 if p < 64 else 1.0
        ang = (t.astype(np.float32) * f).astype(np.float32)
        cosR[p, :TL] = np.cos(ang)
        sinR[p, :TL] = sgn * np.sin(ang)
    j = np.arange(128)[:, None].astype(np.float32)
    i = np.arange(128)[None, :].astype(np.float32)
    dpos = np.maximum(i - j, 0)
    dneg = np.maximum(j - i, 0)
    triu = (i >= j).astype(np.float32)
    tril = (j >= i).astype(np.float32)
    idx1 = np.broadcast_to(i + 1, (128, 128))
    idx2 = np.broadcast_to(128 - i, (128, 128))
    cret = np.stack([np.tile(a, (1, 4)) for a in [dpos, dneg, triu, tril, idx1, idx2]]).astype(np.float32)
    kj = np.stack([127 - np.arange(128), np.arange(128)], axis=1).astype(np.float32)
    rcnt = np.zeros((4, TT), np.float32)
    for g, w in enumerate((2, 4, 8, 16)):
        for (s0, n) in [(0, TL), (TL, 256), (TL + 256, 256)]:
            tt_ = np.arange(n)
            lo = np.clip(tt_ - w // 2, 0, n)
            hi = np.clip(tt_ + w // 2, 0, n)
            rcnt[g, s0:s0 + n] = 1.0 / (hi - lo)
    return dict(c_ident=np.eye(128, dtype=np.float32), c_cosR=cosR, c_sinR=sinR, c_cosM=cosM, c_sinM=sinM,
                c_ret=cret, c_kj=kj, c_rcnt=rcnt)


_CACHE = {}


def kernel(x_prompt, x_sample, cache_mla_ckv, cache_mla_krope, state_ret, c, c_ctx,
           w_ada, b_ada, norm_pre, norm_post, ffn1_w_gu, ffn1_w_down, ffn2_w_gu, ffn2_w_down,
           w_in, mla_q_norm, mla_w_uq, mla_kv_norm, mla_w_ukv, ret_decay, pool_w, pool_scale,
           w_branch_attn, w_branch_ret, w_branch_pool, w_out, _debug_stop=None):
    f = lambda a: np.ascontiguousarray(np.asarray(a, dtype=np.float32))
    if "nc" not in _CACHE or _CACHE.get("stop") != _debug_stop:
        _CACHE["nc"] = build_program(_debug_stop)
        _CACHE["stop"] = _debug_stop
        _CACHE["tables"] = _tables()
    nc = _CACHE["nc"]
    tabs = _CACHE["tables"]
    vecs = np.concatenate([f(b_ada).reshape(L, 72, 128), f(norm_pre).reshape(L, 24, 128), f(norm_post).reshape(L, 24, 128),
                           f(mla_q_norm).reshape(L, 2, 128), f(mla_kv_norm).reshape(L, 1, 128), f(pool_scale).reshape(L, 4, 128)], axis=1)
    shared = dict(vecs=f(vecs), kvn_row=f(mla_kv_norm), ret_decay=f(ret_decay).reshape(L, 8), w_ada=f(w_ada),
                  ffn1_w_gu=f(ffn1_w_gu), ffn2_w_gu=f(ffn2_w_gu), ffn1_w_down=f(ffn1_w_down), ffn2_w_down=f(ffn2_w_down),
                  w_in=f(w_in), mla_w_uq=f(mla_w_uq), mla_w_ukv=f(mla_w_ukv), pool_w=f(pool_w).reshape(L, 512, 128),
                  w_branch_attn=f(w_branch_attn), w_branch_ret=f(w_branch_ret), w_branch_pool=f(w_branch_pool), w_out=f(w_out))
    shared.update(tabs)
    xs = f(x_sample)
    xpr = f(x_prompt)
    in_maps = []
    for core in range(NCORES):
        b = core % 4
        m = dict(shared)
        m["xl"] = xs[b]
        m["xp"] = xpr[2 * core:2 * core + 2].reshape(TP, D)
        m["cckv"] = f(cache_mla_ckv)[b]
        m["ckr"] = f(cache_mla_krope)[b]
        m["st0"] = f(state_ret)[b]
        m["cv"] = np.concatenate([f(c)[b].reshape(8, 128), f(c_ctx).reshape(8, 128)], axis=0)
        in_maps.append(m)
    res = run_bass_kernel_spmd(nc, in_maps, core_ids=list(range(NCORES)))
    R = res.results
    y_sample = np.stack([R[b]["yl"] for b in range(4)], axis=0)
    y_prompt = np.concatenate([R[cc]["yp"].reshape(2, 256, D) for cc in range(NCORES)], axis=0)
    n_ckv = np.concatenate([R[cc]["ockv"] for cc in range(NCORES)], axis=0)
    n_kr = np.concatenate([R[cc]["okr"] for cc in range(NCORES)], axis=0)
    n_st = np.concatenate([R[cc]["ost"] for cc in range(NCORES)], axis=0)
    return (y_prompt.astype(np.float32), y_sample.astype(np.float32), n_ckv.astype(np.float32),
            n_kr.astype(np.float32), n_st.astype(np.float32))
```

```python
import math
import os
import numpy as np
RET_STOP = float(os.environ.get('RET_STOP', '99'))
from contextlib import ExitStack
import concourse.bass as bass
import concourse.mybir as mybir
from concourse.bass_utils import run_bass_kernel_spmd

F32 = mybir.dt.float32
BF16 = mybir.dt.bfloat16
AF = mybir.ActivationFunctionType
ALU = mybir.AluOpType

D = 1024
L = 2
DFF = 2816
TL = 4096
TP = 512
TT = TL + TP
PAST = 512
EPS = 1e-6
INW = 6048
O_GA, O_GB, O_GC, O_QC, O_KVC, O_KR, O_RQ, O_RK, O_RV, O_RG, O_PU = 0, 1024, 2048, 3072, 3328, 3456, 3488, 4000, 4512, 5024, 5536
NCORES = 8
SLOT = 5632
NSLOT = 3

V_BADA, V_NPRE, V_NPOST, V_QN, V_KVN, V_PS = 0, 72, 96, 120, 122, 123
NVEC = 127


class Tk:
    __slots__ = ("name", "w", "r", "sem", "cnt", "persist")

    def __init__(self, name, persist=False):
        self.persist = persist
        self.name = name
        self.w = {}
        self.r = {}
        self.sem = None
        self.cnt = 0


class KL(list):
    pass


def _flat(tks):
    out = []
    for t in tks:
        if isinstance(t, (list, tuple)):
            out.extend(_flat(t))
        else:
            out.append(t)
    return out


class Trk:
    ENG = ("pe", "act", "dve", "pool", "sp")

    def __init__(self, nc, es):
        self.nc = nc
        self.es = es
        self.e = {"pe": nc.tensor, "act": nc.scalar, "dve": nc.vector, "pool": nc.gpsimd, "sp": nc.sync}
        self.sem = {k: es.enter_context(nc.semaphore("s_" + k)) for k in self.ENG}
        self.seq = {k: 0 for k in self.ENG}
        self.seen = {k: {} for k in self.ENG}
        self.semobj = dict(self.sem)
        self.nsem = 0
        self.ninst = 0
        self.pool_sems = []
        self.pool_cnt = {}
        self.free_keys = {"sp": [], "pool": []}
        self.phase_keys = []
        self.key_q = {}

    NPOOL = 90

    def _dsem(self, tk, eng):
        if tk.sem is None:
            tk.sem = {}
        if eng not in tk.sem:
            if len(self.pool_sems) < self.NPOOL:
                h = self.es.enter_context(self.nc.semaphore("d%d" % len(self.pool_sems)))
                key = ("d", len(self.pool_sems))
                self.pool_sems.append(key)
                self.semobj[key] = h
                self.pool_cnt[key] = 0
                self.key_q[key] = eng
            elif self.free_keys[eng]:
                key = self.free_keys[eng].pop()
            else:
                cands = [k for k in self.pool_sems if self.key_q[k] == eng]
                key = cands[self.nsem % len(cands)]
            self.nsem += 1
            tk.sem[eng] = key
            if not getattr(tk, "persist", False):
                self.phase_keys.append(key)
        return tk.sem[eng]

    def end_phase(self):
        for k in self.phase_keys:
            q = self.key_q[k]
            if k not in self.free_keys[q]:
                self.free_keys[q].append(k)
        self.phase_keys = []

    def _wait(self, eng, deps):
        seen = self.seen[eng]
        E = self.e[eng]
        for k, v in deps.items():
            if eng == "pe" and k == "pe":
                continue
            if k in self.pool_cnt:
                v = self.pool_cnt[k]
            if seen.get(k, 0) < v:
                E.wait_ge(self.semobj[k], v)
                seen[k] = v
                self.ninst += 1

    def _deps(self, reads, writes):
        d = {}
        for t in reads:
            for k, v in t.w.items():
                if d.get(k, 0) < v:
                    d[k] = v
        for t in writes:
            for k, v in t.w.items():
                if d.get(k, 0) < v:
                    d[k] = v
            for k, v in t.r.items():
                if d.get(k, 0) < v:
                    d[k] = v
        return d

    def op(self, eng, fn, reads=(), writes=(), inc=True):
        reads = _flat(reads)
        writes = _flat(writes)
        self._wait(eng, self._deps(reads, writes))
        ins = fn(self.e[eng])
        self.ninst += 1
        v = self.seq[eng] + 1
        if inc:
            ins.then_inc(self.sem[eng], 1)
            self.seq[eng] = v
        for t in writes:
            if t.w.get(eng, 0) < v:
                t.w[eng] = v
        for t in reads:
            if t.r.get(eng, 0) < v:
                t.r[eng] = v
        return ins

    def dma(self, eng, out, in_, reads, write):
        reads = _flat(reads)
        wl = _flat([write])
        write = wl[0]
        self._wait(eng, self._deps(reads, wl))
        k = self._dsem(write, eng)
        self.e[eng].dma_start(out=out, in_=in_).then_inc(self.semobj[k], 16)
        self.ninst += 1
        self.pool_cnt[k] += 16
        v = self.pool_cnt[k]
        for t in wl:
            t.w[k] = v
        for t in reads:
            t.r[k] = v

    def barrier(self, tks=()):
        for eng in self.ENG:
            d = {k: self.seq[k] for k in self.ENG if k != eng}
            for k, v in self.pool_cnt.items():
                d[k] = v
            self._wait(eng, d)


def build_program(debug_stop=None):
    nc = bass.Bass("TRN2", target_bir_lowering=False)
    es = ExitStack()
    with es:
        T = Trk(nc, es)

        def din(name, shape, dt=F32):
            return nc.dram_tensor(name, list(shape), dt, kind="ExternalInput").ap()

        def dout(name, shape, dt=F32):
            return nc.dram_tensor(name, list(shape), dt, kind="ExternalOutput").ap()

        def dscr(name, shape, dt):
            return nc.dram_tensor(name, list(shape), dt, kind="Internal").ap()

        xl = din("xl", [TL, D])
        xp = din("xp", [TP, D])
        cckv = din("cckv", [L, PAST, 128])
        ckr = din("ckr", [L, PAST, 32])
        st0 = din("st0", [L, 2, 4, 128, 128])
        cv = din("cv", [16, 128])
        vecs = din("vecs", [L, NVEC, 128])
        kvn_row = din("kvn_row", [L, 128])
        ret_decay = din("ret_decay", [L, 8])
        w_ada = din("w_ada", [L, D, 9 * D])
        w_gu = [din("ffn1_w_gu", [L, D, 2 * DFF]), din("ffn2_w_gu", [L, D, 2 * DFF])]
        w_dn = [din("ffn1_w_down", [L, DFF, D]), din("ffn2_w_down", [L, DFF, D])]
        w_in = din("w_in", [L, D, INW])
        w_uq = din("mla_w_uq", [L, 256, 768])
        w_ukv = din("mla_w_ukv", [L, 128, 1024])
        pool_w = din("pool_w", [L, 512, 128])
        w_ba = din("w_branch_attn", [L, 512, D])
        w_br = din("w_branch_ret", [L, 512, D])
        w_bp = din("w_branch_pool", [L, 512, D])
        w_out = din("w_out", [L, D, D])
        c_ident = din("c_ident", [128, 128])
        c_cosR = din("c_cosR", [128, TT])
        c_sinR = din("c_sinR", [128, TT])
        c_cosM = din("c_cosM", [128, TT])
        c_sinM = din("c_sinM", [128, TT])
        c_ret = din("c_ret", [6, 128, 512])
        c_kj = din("c_kj", [128, 2])
        c_rcnt = din("c_rcnt", [4, TT])

        yl = dout("yl", [TL, D])
        yp = dout("yp", [TP, D])
        ockv = dout("ockv", [2, L, 256, 128])
        okr = dout("okr", [2, L, 256, 32])
        ost = dout("ost", [2, L, 2, 4, 128, 128])

        gu_b = [[dscr("gu_b%d_%d" % (f, l), [D, 2 * DFF], BF16) for l in range(L)] for f in range(2)]
        dn_b = [[dscr("dn_b%d_%d" % (f, l), [DFF, D], BF16) for l in range(L)] for f in range(2)]
        in_b = [dscr("in_b%d" % l, [D, INW], BF16) for l in range(L)]
        inr_b = [dscr("inr_b%d" % l, [D, 1056], BF16) for l in range(L)]
        uq_b = [dscr("uq_b%d" % l, [256, 768], BF16) for l in range(L)]
        uqr_b = [dscr("uqr_b%d" % l, [256, 768], BF16) for l in range(L)]
        ukv_b = [dscr("ukv_b%d" % l, [128, 1024], BF16) for l in range(L)]
        pw_b = [dscr("pw_b%d" % l, [512, 128], BF16) for l in range(L)]
        ba_b = [dscr("ba_b%d" % l, [512, D], BF16) for l in range(L)]
        br_b = [dscr("br_b%d" % l, [512, D], BF16) for l in range(L)]
        bp_b = [dscr("bp_b%d" % l, [512, D], BF16) for l in range(L)]
        out_b = [dscr("out_b%d" % l, [D, D], BF16) for l in range(L)]
        XT1 = dscr("XT1", [8, 128, TT], F32)
        X1T = [dscr("X1T%d" % l, [8, 128, TT], F32) for l in range(L)]
        SG = [dscr("SG%d" % l, [24, 128, TT], BF16) for l in range(L)]
        QCN = [dscr("QCN%d" % l, [2, 128, TT], BF16) for l in range(L)]
        CKVT = [dscr("CKVT%d" % l, [128, TT], BF16) for l in range(L)]
        KRT = [dscr("KRT%d" % l, [32, TT], BF16) for l in range(L)]
        RQT = [dscr("RQT%d" % l, [4, 128, TT], BF16) for l in range(L)]
        RKT = [dscr("RKT%d" % l, [4, 128, TT], BF16) for l in range(L)]
        RGS = [dscr("RGS%d" % l, [4, 128, TT], BF16) for l in range(L)]
        RV = [dscr("RV%d" % l, [4, 128, TT // 128, 128], BF16) for l in range(L)]
        PUT = [dscr("PUT%d" % l, [4, 128, TT], F32) for l in range(L)]
        OA = [dscr("OA%d" % l, [512, TT], BF16) for l in range(L)]
        OB = [dscr("OB%d" % l, [4, 128, TT], BF16) for l in range(L)]
        OC = [dscr("OC%d" % l, [4, 128, TT], BF16) for l in range(L)]
        tk = {}

        def TKn(name):
            if name not in tk:
                tk[name] = Tk(name, True)
            return tk[name]

        def sb(name, shape, dt):
            return es.enter_context(nc.sbuf_tensor(name, list(shape), dt))

        ident = sb("ident", [128, 128], F32)
        ones_f = sb("ones_f", [128, 128], F32)
        ones_b = sb("ones_b", [128, 128], BF16)
        MV = sb("MV", [128, L * 2 * 3 * 3 * 8], F32)
        VT = sb("VT", [128, L * 128], F32)
        kvn_bc = sb("kvn_bc", [128, L * 128], F32)
        ring = [sb("ring%d" % i, [128, SLOT], BF16) for i in range(NSLOT)]
        ring_tk = [Tk("ring%d" % i, True) for i in range(NSLOT)]
        ring_pos = [0]
        psb = [es.enter_context(nc.psum_tensor("psb%d" % i, [128, 512], F32)) for i in range(8)]
        ps_tk = [Tk("ps%d" % i) for i in range(8)]
        k_const = Tk("const", True)
        k_out = [Tk("o_yl", True), Tk("o_yp", True), Tk("o_ckv", True), Tk("o_kr", True), Tk("o_st", True)]

        def MVs(l, g, s, abc):
            o = (((l * 2 + g) * 3 + s) * 3 + abc) * 8
            return MV[:, o:o + 8]

        def vt(l, row0, n):
            return VT[:, l * 128 + row0: l * 128 + row0 + n]

        def conv(dst, src, rows, name, nsplit=1):
            k = TKn(name)
            step = rows // nsplit
            for i in range(nsplit):
                T.dma("pool", dst[i * step:(i + 1) * step, :], src[i * step:(i + 1) * step, :], [], k)
            return k

        def conv_layer(l):
            conv(gu_b[0][l], w_gu[0][l], D, "gu0_%d" % l, 4)
            conv(dn_b[0][l], w_dn[0][l], DFF, "dn0_%d" % l, 2)
            conv(in_b[l], w_in[l], D, "in_%d" % l, 4)
            k = TKn("inr_%d" % l)
            for (d0, s0, n) in [(0, O_KR + 8, 8), (8, O_KR, 8), (16, O_KR + 24, 8), (24, O_KR + 16, 8)]:
                T.dma("pool", inr_b[l][:, d0:d0 + n], w_in[l][:, s0:s0 + n], [], k)
            for base_d, base_s in [(32, O_RQ), (544, O_RK)]:
                srcv = w_in[l][:, base_s:base_s + 512].rearrange("k (h t d) -> k h t d", h=4, t=2)
                dstv = inr_b[l][:, base_d:base_d + 512].rearrange("k (h t d) -> k h t d", h=4, t=2)
                for h in range(4):
                    T.dma("pool", dstv[:, h, 0, :], srcv[:, h, 1, :], [], k)
                    T.dma("pool", dstv[:, h, 1, :], srcv[:, h, 0, :], [], k)
            conv(uq_b[l], w_uq[l], 256, "uq_%d" % l)
            k = TKn("uqr_%d" % l)
            sv = w_uq[l].rearrange("k (h c) -> k h c", h=8)
            dv = uqr_b[l].rearrange("k (h c) -> k h c", h=8)
            for (d0, s0) in [(64, 72), (72, 64), (80, 88), (88, 80)]:
                T.dma("pool", dv[:, :, d0:d0 + 8], sv[:, :, s0:s0 + 8], [], k)
            T.dma("pool", dv[:, :, 0:64], sv[:, :, 0:64], [], k)
            conv(ukv_b[l], w_ukv[l], 128, "ukv_%d" % l)
            conv(pw_b[l], pool_w[l], 512, "pw_%d" % l)
            conv(ba_b[l], w_ba[l], 512, "ba_%d" % l)
            conv(br_b[l], w_br[l], 512, "br_%d" % l)
            conv(bp_b[l], w_bp[l], 512, "bp_%d" % l)
            conv(out_b[l], w_out[l], D, "out_%d" % l, 2)
            conv(gu_b[1][l], w_gu[1][l], D, "gu1_%d" % l, 4)
            conv(dn_b[1][l], w_dn[1][l], DFF, "dn1_%d" % l, 2)

        T.dma("sp", ident[:], c_ident[:, :], [], k_const)
        T.op("dve", lambda E: E.memset(ones_f[:], 1.0), [], [k_const])
        T.op("dve", lambda E: E.memset(ones_b[:], 1.0), [], [k_const])

        def ring_load(src_aps, reads):
            i = ring_pos[0] % NSLOT
            ring_pos[0] += 1
            for ap_, off, kc, ncols in src_aps:
                dst = ring[i][:, off:off + kc * ncols].rearrange("p (k n) -> p k n", k=kc)
                T.dma("sp", dst, ap_, reads, ring_tk[i])
            return i

        def wsrc(Wb, k0, kc, c0, ncols):
            return Wb[k0 * 128:(k0 + kc) * 128, c0:c0 + ncols].rearrange("(k p) n -> p k n", p=128)

        def slot_view(i, off, kc, ncols):
            return ring[i][:, off:off + kc * ncols].rearrange("p (k n) -> p k n", k=kc)

        def mm_group(banks, M, lhsT_fn, rhs_fn, kc, nsub, reads, sub=512, first=True, last=True):
            for k in range(kc):
                rk = [x[k] if isinstance(x, KL) else x for x in reads]
                for s in range(nsub):
                    b = banks[s]
                    T.op("pe", lambda E, k=k, s=s, b=b: E.matmul(
                        psb[b][0:M, 0:sub], lhsT=lhsT_fn(k), rhs=rhs_fn(k, s),
                        start=(first and k == 0), stop=(last and k == kc - 1)),
                        rk, [ps_tk[b]], inc=(k == kc - 1))

        def ps2(banks, M, n):
            out = []
            for s, b in enumerate(banks):
                w = min(512, n - s * 512)
                if w <= 0:
                    break
                out.append((psb[b][0:M, 0:w], ps_tk[b], s * 512, w))
            return out

        with ExitStack() as es0:
            def sb0(name, shape, dt):
                return es0.enter_context(nc.sbuf_tensor(name, list(shape), dt))
            vraw = sb0("vraw", [128, 128], F32)
            cvraw = sb0("cvraw", [16, 128], F32)
            scv = sb0("scv", [128, 16], BF16)
            wst = [sb0("wst%d" % j, [128, 4096], F32) for j in range(4)]
            wbf = [sb0("wbf%d" % j, [128, 4096], BF16) for j in range(2)]
            k_wst = [Tk("wst%d" % j) for j in range(4)]
            k_wbf = [Tk("wbf%d" % j) for j in range(2)]
            modT = sb0("modT", [128, 144], F32)
            k_vraw, k_cv, k_scv, k_modT, k_VT, k_MV = Tk("vraw"), Tk("cvraw"), Tk("scv"), Tk("modT"), TKn("VT"), TKn("MV")
            k_wada = Tk("wada")
            T.dma("sp", cvraw[:], cv[:, :], [], k_cv)
            T.op("pe", lambda E: E.transpose(psb[0][:, 0:16], cvraw[:], ident[0:16, 0:16]), [k_cv, k_const], [ps_tk[0]])
            T.op("act", lambda E: E.activation(out=scv[:], in_=psb[0][:, 0:16], func=AF.Silu), [ps_tk[0]], [k_scv])
            for l in range(L):
                T.op("dve", lambda E: E.memset(vraw[:], 0.0), [], [k_vraw])
                T.dma("sp", vraw[0:NVEC, :], vecs[l], [], k_vraw)
                T.op("pe", lambda E: E.transpose(psb[1][:, 0:128], vraw[:], ident[:]), [k_vraw, k_const], [ps_tk[1]])
                T.op("act", lambda E, l=l: E.copy(out=VT[:, l * 128:(l + 1) * 128], in_=psb[1][:, 0:128]), [ps_tk[1]], [k_VT])
                T.dma("sp", kvn_bc[:, l * 128:(l + 1) * 128], kvn_row[l:l + 1, :].partition_broadcast(128), [], k_const)
                for u in range(18):
                    i = u % 4
                    T.dma("sp", wst[i][:, :].rearrange("p (k n) -> p k n", k=8), wsrc(w_ada[l], 0, 8, u * 512, 512), [], k_wst[i])
                    i2 = u % 2
                    T.op("dve", lambda E, i=i, i2=i2: E.tensor_copy(out=wbf[i2][:, 0:2048], in_=wst[i][:, 0:2048]), [k_wst[i]], [k_wbf[i2]])
                    T.op("act", lambda E, i=i, i2=i2: E.copy(out=wbf[i2][:, 2048:4096], in_=wst[i][:, 2048:4096]), [k_wst[i]], [k_wbf[i2]])
                    rf = wbf[i2][:, :].rearrange("p (k n) -> p k n", k=8)
                    for c in range(4):
                        blk = u * 4 + c
                        for k in range(8):
                            T.op("pe", lambda E, k=k, c=c, blk=blk, rf=rf: E.matmul(
                                psb[2][:, blk * 2:blk * 2 + 2], lhsT=rf[:, k, c * 128:(c + 1) * 128],
                                rhs=scv[:, k:16:8], start=(k == 0), stop=(k == 7)),
                                [k_wbf[i2], k_scv], [ps_tk[2]], inc=(k == 7))
                T.op("act", lambda E: E.copy(out=modT[:], in_=psb[2][:, 0:144]), [ps_tk[2]], [k_modT])
                mT = modT[:].rearrange("p (b g) -> p b g", g=2)
                for g in range(2):
                    T.op("dve", lambda E, g=g, l=l: E.tensor_tensor(out=mT[:, :, g], in0=mT[:, :, g], in1=vt(l, V_BADA, 72), op=ALU.add),
                         [k_modT, k_VT], [k_modT])
                for g in range(2):
                    for s in range(3):
                        sh = mT[:, (3 * s) * 8:(3 * s) * 8 + 8, g]
                        sc = mT[:, (3 * s + 1) * 8:(3 * s + 1) * 8 + 8, g]
                        gt = mT[:, (3 * s + 2) * 8:(3 * s + 2) * 8 + 8, g]
                        T.op("dve", lambda E, sc=sc, l=l, g=g, s=s: E.scalar_tensor_tensor(
                            out=MVs(l, g, s, 0), in0=sc, scalar=1.0, in1=vt(l, V_NPRE + 8 * s, 8), op0=ALU.add, op1=ALU.mult),
                            [k_modT, k_VT], [k_MV])
                        T.op("dve", lambda E, sh=sh, l=l, g=g, s=s: E.tensor_copy(out=MVs(l, g, s, 1), in_=sh), [k_modT], [k_MV])
                        T.op("dve", lambda E, gt=gt, l=l, g=g, s=s: E.scalar_tensor_tensor(
                            out=MVs(l, g, s, 2), in0=gt, scalar=(1.0 if s == 1 else 0.5), in1=vt(l, V_NPOST + 8 * s, 8),
                            op0=ALU.mult, op1=ALU.mult), [k_modT, k_VT], [k_MV])
            T.barrier([k_const, k_vraw, k_cv] + ring_tk)
            T.end_phase()
        k_VT, k_MV = tk["VT"], tk["MV"]

        for l in range(L):
            conv_layer(l)

        STS = [(0, 1024, 0), (1024, 1024, 0), (2048, 1024, 0), (3072, 1024, 0), (4096, 512, 1)]

        def rms_stats(pieces_fn, nchunks, n, nfeat, rstd, k_rstd, sqt, k_sq, reads):
            nsub = (n + 511) // 512
            for c in range(nchunks):
                ap_, tk_ = pieces_fn(c)
                j = c % 2
                T.op("act", lambda E, ap_=ap_, j=j: E.activation(out=sqt[j][:, 0:n], in_=ap_, func=AF.Square), [tk_] + reads, [k_sq[j]])
                for s in range(nsub):
                    w = min(512, n - s * 512)
                    T.op("pe", lambda E, s=s, w=w, j=j, c=c: E.matmul(psb[s][:, 0:w], lhsT=ones_b[:], rhs=sqt[j][:, s * 512:s * 512 + w],
                                                                   start=(c == 0), stop=(c == nchunks - 1)),
                         [k_sq[j], k_const], [ps_tk[s]], inc=True)
            for s in range(nsub):
                w = min(512, n - s * 512)
                T.op("act", lambda E, s=s, w=w: E.activation(out=rstd[:, s * 512:s * 512 + w], in_=psb[s][:, 0:w], func=AF.Sqrt,
                                                           scale=1.0 / nfeat, bias=eps_t[:, 0:1]), [ps_tk[s], k_const], [k_rstd])
            T.op("dve", lambda E: E.reciprocal(out=rstd[:, 0:n], in_=rstd[:, 0:n]), [k_rstd], [k_rstd])

        eps_t = sb("eps_t", [128, 1], F32)
        T.op("dve", lambda E: E.memset(eps_t[:], EPS), [], [k_const])

        def run_rowlocal(l, phase, esx):
            def sbx(name, shape, dt):
                return esx.enter_context(nc.sbuf_tensor(name + "_%d%d" % (l, phase), list(shape), dt))
            xT = sbx("xT", [128, 8, 1024], F32)
            yT = sbx("yT", [128, 8, 1024], F32)
            hT = sbx("hT", [128, 8, 1024], BF16)
            aT = sbx("aT", [128, 22, 1024], BF16)
            rstd = sbx("rstd", [128, 1024], F32)
            sqt = [sbx("sq%d" % j, [128, 1024], BF16) for j in range(2)]
            tmp = [sbx("tmp%d" % j, [128, 1024], F32) for j in range(2)]
            stg = [sbx("stg%d" % j, [128, 1024], BF16) for j in range(3)]
            tab = [sbx("tab%d" % j, [128, 1024], F32) for j in range(2)]
            k_xT, k_yT, k_hT = KL(Tk("xT%d" % i) for i in range(8)), KL(Tk("yT%d" % i) for i in range(8)), KL(Tk("hT%d" % i) for i in range(8))
            k_aT, k_rstd = Tk("aT"), Tk("rstd")
            k_sq = [Tk("sq0"), Tk("sq1")]
            k_tmp = [Tk("tmp0"), Tk("tmp1")]
            k_stg = [Tk("stg%d" % j) for j in range(3)]
            k_tab = [Tk("tab%d" % j) for j in range(2)]
            cnt = {"tmp": 0, "stg": 0, "yb": 0}

            def nxt(key, n):
                i = cnt[key] % n
                cnt[key] += 1
                return i

            def prenorm(n, g, s):
                rms_stats(lambda c: (xT[:, c, 0:n], k_xT[c]), 8, n, D, rstd, k_rstd, sqt, k_sq, [])
                A = MVs(l, g, s, 0)
                B = MVs(l, g, s, 1)
                for c in range(8):
                    j = nxt("tmp", 2)
                    T.op("dve", lambda E, c=c, j=j: E.scalar_tensor_tensor(out=tmp[j][:, 0:n], in0=xT[:, c, 0:n], scalar=A[:, c:c + 1], in1=rstd[:, 0:n],
                                                                           op0=ALU.mult, op1=ALU.mult), [k_xT[c], k_rstd, k_MV], [k_tmp[j]])
                    T.op("act", lambda E, c=c, j=j: E.activation(out=hT[:, c, 0:n], in_=tmp[j][:, 0:n], func=AF.Identity, bias=B[:, c:c + 1]),
                         [k_tmp[j], k_MV], [k_hT[c]])

            def post_resid(n, g, s):
                rms_stats(lambda c: (yT[:, c, 0:n], k_yT[c]), 8, n, D, rstd, k_rstd, sqt, k_sq, [])
                C = MVs(l, g, s, 2)
                for c in range(8):
                    j = nxt("tmp", 2)
                    T.op("pool", lambda E, c=c, j=j: E.tensor_tensor(out=tmp[j][:, 0:n], in0=yT[:, c, 0:n], in1=rstd[:, 0:n], op=ALU.mult), [k_yT[c], k_rstd], [k_tmp[j]])
                    T.op("dve", lambda E, c=c, j=j: E.scalar_tensor_tensor(out=xT[:, c, 0:n], in0=tmp[j][:, 0:n], scalar=C[:, c:c + 1], in1=xT[:, c, 0:n],
                                                                           op0=ALU.mult, op1=ALU.add), [k_tmp[j], k_xT[c], k_MV], [k_xT[c]])

            def ffn(n, g, s, f):
                nsub = n // 512
                prenorm(n, g, s)
                Wg = gu_b[f][l]
                kW = tk["gu%d_%d" % (f, l)]
                for u in range(11):
                    i = ring_load([(wsrc(Wg, 0, 8, u * 256, 256), 0, 8, 256), (wsrc(Wg, 0, 8, DFF + u * 256, 256), 2048, 8, 256)], [kW])
                    gv = slot_view(i, 0, 8, 256)
                    uv = slot_view(i, 2048, 8, 256)
                    for jj in range(2):
                        j = u * 2 + jj
                        mm_group([0, 1], 128, lambda k: gv[:, k, jj * 128:(jj + 1) * 128], lambda k, s_: hT[:, k, s_ * 512:(s_ + 1) * 512], 8, nsub, [ring_tk[i], k_hT])
                        mm_group([2, 3], 128, lambda k: uv[:, k, jj * 128:(jj + 1) * 128], lambda k, s_: hT[:, k, s_ * 512:(s_ + 1) * 512], 8, nsub, [ring_tk[i], k_hT])
                        t = nxt("tmp", 2)
                        for s_ in range(nsub):
                            T.op("act", lambda E, s_=s_, t=t: E.activation(out=tmp[t][:, s_ * 512:(s_ + 1) * 512], in_=psb[s_][:, :], func=AF.Silu),
                                 [ps_tk[s_]], [k_tmp[t]])
                        for s_ in range(nsub):
                            T.op("dve", lambda E, s_=s_, t=t, j=j: E.tensor_tensor(out=aT[:, j, s_ * 512:(s_ + 1) * 512], in0=tmp[t][:, s_ * 512:(s_ + 1) * 512],
                                                                                   in1=psb[2 + s_][:, :], op=ALU.mult), [k_tmp[t], ps_tk[2 + s_]], [k_aT])
                Wd = dn_b[f][l]
                kWd = tk["dn%d_%d" % (f, l)]
                for u in range(4):
                    i = ring_load([(wsrc(Wd, 0, 22, u * 256, 256), 0, 22, 256)], [kWd])
                    dvw = slot_view(i, 0, 22, 256)
                    for jj in range(2):
                        dc = u * 2 + jj
                        yb = [4, 5] if nxt("yb", 2) == 0 else [6, 7]
                        mm_group(yb, 128, lambda k: dvw[:, k, jj * 128:(jj + 1) * 128], lambda k, s_: aT[:, k, s_ * 512:(s_ + 1) * 512], 22, nsub, [ring_tk[i], k_aT])
                        for s_ in range(nsub):
                            T.op("act", lambda E, s_=s_, dc=dc, b=yb[s_]: E.copy(out=yT[:, dc, s_ * 512:(s_ + 1) * 512], in_=psb[b][:, :]), [ps_tk[yb[s_]]], [k_yT[dc]])
                post_resid(n, g, s)

            def load_x(t0, n, g):
                if l == 0 and phase == 1:
                    src = xl if g == 0 else xp
                    r0 = t0 if g == 0 else t0 - TL
                    for blk in range(n // 128):
                        j = nxt("tmp", 2)
                        T.dma("sp", tmp[j][:, :], src[r0 + blk * 128:r0 + (blk + 1) * 128, :], [], k_tmp[j])
                        yb = [4, 5] if nxt("yb", 2) == 0 else [6, 7]
                        for c in range(8):
                            b = yb[c // 4]
                            T.op("pe", lambda E, c=c, b=b, j=j: E.transpose(psb[b][:, (c % 4) * 128:(c % 4 + 1) * 128], tmp[j][:, c * 128:(c + 1) * 128], ident[:]),
                                 [k_tmp[j], k_const], [ps_tk[b]])
                        for hh in range(2):
                            b = yb[hh]
                            T.op("act" if hh == 0 else "dve", lambda E, hh=hh, b=b, blk=blk: (E.copy if hh == 0 else E.tensor_copy)(
                                out=xT[:, hh * 4:(hh + 1) * 4, blk * 128:(blk + 1) * 128], in_=psb[b][:, :].rearrange("p (c t) -> p c t", c=4)),
                                [ps_tk[b]], [k_xT[hh * 4:(hh + 1) * 4]])
                else:
                    src = XT1 if phase == 1 else X1T[l]
                    ksrc = TKn("XT1") if phase == 1 else TKn("X1T%d" % l)
                    T.dma("sp", xT[:, :, 0:n], src[:, :, t0:t0 + n].rearrange("c p t -> p c t"), [ksrc], k_xT)

            def store_fm(dst3, kdst, t0, n):
                T.dma("pool", dst3[:, :, t0:t0 + n].rearrange("c p t -> p c t"), xT[:, :, 0:n], [k_xT], kdst)

            def store_final(t0, n, g):
                dst = yl if g == 0 else yp
                kd = k_out[0] if g == 0 else k_out[1]
                r0 = t0 if g == 0 else t0 - TL
                for blk in range(n // 128):
                    yb = [4, 5] if nxt("yb", 2) == 0 else [6, 7]
                    for c in range(8):
                        b = yb[c // 4]
                        T.op("pe", lambda E, c=c, b=b, blk=blk: E.transpose(psb[b][:, (c % 4) * 128:(c % 4 + 1) * 128], xT[:, c, blk * 128:(blk + 1) * 128], ident[:]),
                             [k_xT[c], k_const], [ps_tk[b]])
                    j = nxt("tmp", 2)
                    T.op("act", lambda E, j=j, b=yb[0]: E.copy(out=tmp[j][:, 0:512], in_=psb[b][:, :]), [ps_tk[yb[0]]], [k_tmp[j]])
                    T.op("dve", lambda E, j=j, b=yb[1]: E.tensor_copy(out=tmp[j][:, 512:1024], in_=psb[b][:, :]), [ps_tk[yb[1]]], [k_tmp[j]])
                    T.dma("pool", dst[r0 + blk * 128:r0 + (blk + 1) * 128, :], tmp[j][:, :], [k_tmp[j]], kd)

            def evac_store(pieces, func, dst2, kdst, t0, dt_bf=True, scale=None):
                j = nxt("stg", 3)
                n = 0
                for (ap_, tk_, c0, w) in pieces:
                    T.op("act", lambda E, ap_=ap_, c0=c0, w=w, j=j: E.activation(out=stg[j][0:ap_.shape[0], c0:c0 + w], in_=ap_, func=func), [tk_], [k_stg[j]])
                    n = c0 + w
                m = pieces[0][0].shape[0]
                T.dma("pool", dst2[:, t0:t0 + n], stg[j][0:m, 0:n], [k_stg[j]], kdst)

            def proj(l, t0, n, g, after_gates=None):
                nsub = n // 512
                Wi = in_b[l]
                kWi = tk["in_%d" % l]
                kWr = tk["inr_%d" % l]
                rhs = lambda k, s_: hT[:, k, s_ * 512:(s_ + 1) * 512]
                for j_, src in enumerate([c_cosM, c_sinM]):
                    T.dma("sp", tab[j_][:, 0:n], src[:, t0:t0 + n], [], k_tab[j_])
                for u in range(6):
                    i = ring_load([(wsrc(Wi, 0, 8, u * 512, 512), 0, 8, 512)], [kWi])
                    v = slot_view(i, 0, 8, 512)
                    for c in range(4):
                        yb = [4, 5] if nxt("yb", 2) == 0 else [6, 7]
                        mm_group(yb, 128, lambda k: v[:, k, c * 128:(c + 1) * 128], rhs, 8, nsub, [ring_tk[i], k_hT])
                        evac_store(ps2(yb, 128, n), AF.Sigmoid, SG[l][u * 4 + c], TKn("SG%d" % l), t0)
                if after_gates is not None:
                    after_gates()
                i = ring_load([(wsrc(Wi, 0, 8, O_QC, 416), 0, 8, 416)], [kWi])
                v = slot_view(i, 0, 8, 416)
                for c in range(2):
                    yb = [4, 5] if nxt("yb", 2) == 0 else [6, 7]
                    mm_group(yb, 128, lambda k: v[:, k, c * 128:(c + 1) * 128], rhs, 8, nsub, [ring_tk[i], k_hT])
                    for (ap_, tk_, c0, w) in ps2(yb, 128, n):
                        T.op("act", lambda E, ap_=ap_, c0=c0, w=w, c=c: E.copy(out=yT[:, c, c0:c0 + w], in_=ap_), [tk_], [k_yT[c]])
                rms_stats(lambda c: (yT[:, c, 0:n], k_yT[c]), 2, n, 256, rstd, k_rstd, sqt, k_sq, [])
                for c in range(2):
                    j = nxt("stg", 3)
                    T.op("dve", lambda E, c=c, j=j: E.scalar_tensor_tensor(out=stg[j][:, 0:n], in0=yT[:, c, 0:n], scalar=vt(l, V_QN + c, 1), in1=rstd[:, 0:n],
                                                                           op0=ALU.mult, op1=ALU.mult), [k_yT[c], k_rstd, k_VT], [k_stg[j]])
                    T.dma("pool", QCN[l][c][:, t0:t0 + n], stg[j][:, 0:n], [k_stg[j]], TKn("QCN%d" % l))
                yb = [4, 5] if nxt("yb", 2) == 0 else [6, 7]
                mm_group(yb, 128, lambda k: v[:, k, 256:384], rhs, 8, nsub, [ring_tk[i], k_hT])
                for (ap_, tk_, c0, w) in ps2(yb, 128, n):
                    T.op("act", lambda E, ap_=ap_, c0=c0, w=w: E.copy(out=yT[:, 2, c0:c0 + w], in_=ap_), [tk_], [k_yT[2]])
                rms_stats(lambda c: (yT[:, 2, 0:n], k_yT[2]), 1, n, 128, rstd, k_rstd, sqt, k_sq, [])
                j = nxt("stg", 3)
                T.op("dve", lambda E, j=j: E.scalar_tensor_tensor(out=stg[j][:, 0:n], in0=yT[:, 2, 0:n], scalar=vt(l, V_KVN, 1), in1=rstd[:, 0:n],
                                                                  op0=ALU.mult, op1=ALU.mult), [k_yT[2], k_rstd, k_VT], [k_stg[j]])
                T.dma("pool", CKVT[l][:, t0:t0 + n], stg[j][:, 0:n], [k_stg[j]], TKn("CKVT%d" % l))
                ir = ring_load([(wsrc(inr_b[l], 0, 8, 0, 32), 0, 8, 32)], [kWr])
                vr = slot_view(ir, 0, 8, 32)
                mm_group([4, 5], 32, lambda k: v[:, k, 384:416], rhs, 8, nsub, [ring_tk[i], k_hT])
                mm_group([6, 7], 32, lambda k: vr[:, k, 0:32], rhs, 8, nsub, [ring_tk[ir], k_hT])
                rope_out(32, [4, 5], [6, 7], tab[0], tab[1], k_tab[0], k_tab[1], n, KRT[l], TKn("KRT%d" % l), t0, 0)
                for j_, src in enumerate([c_cosR, c_sinR]):
                    T.dma("sp", tab[j_][:, 0:n], src[:, t0:t0 + n], [], k_tab[j_])
                for (ocol, rcol, dst, nm) in [(O_RQ, 32, RQT[l], "RQT%d" % l), (O_RK, 544, RKT[l], "RKT%d" % l)]:
                    i = ring_load([(wsrc(Wi, 0, 8, ocol, 512), 0, 8, 512)], [kWi])
                    v = slot_view(i, 0, 8, 512)
                    ir = ring_load([(wsrc(inr_b[l], 0, 8, rcol, 512), 0, 8, 512)], [kWr])
                    vr = slot_view(ir, 0, 8, 512)
                    for c in range(4):
                        mm_group([4, 5], 128, lambda k: v[:, k, c * 128:(c + 1) * 128], rhs, 8, nsub, [ring_tk[i], k_hT])
                        mm_group([6, 7], 128, lambda k: vr[:, k, c * 128:(c + 1) * 128], rhs, 8, nsub, [ring_tk[ir], k_hT])
                        rope_out(128, [4, 5], [6, 7], tab[0], tab[1], k_tab[0], k_tab[1], n, dst[c], TKn(nm), t0, 0)
                i = ring_load([(wsrc(Wi, 0, 8, O_RV, 512), 0, 8, 512)], [kWi])
                v = slot_view(i, 0, 8, 512)
                for blk in range(n // 128):
                    b = 4 + nxt("yb", 4)
                    for k in range(8):
                        T.op("pe", lambda E, k=k, b=b, blk=blk: E.matmul(psb[b][:, :], lhsT=hT[:, k, blk * 128:(blk + 1) * 128], rhs=v[:, k, :], start=(k == 0), stop=(k == 7)),
                             [ring_tk[i], k_hT[k]], [ps_tk[b]], inc=(k == 7))
                    j = nxt("stg", 3)
                    T.op("act", lambda E, j=j, b=b: E.copy(out=stg[j][:, 0:512], in_=psb[b][:, :]), [ps_tk[b]], [k_stg[j]])
                    T.dma("pool", RV[l][:, :, t0 // 128 + blk, :].rearrange("h p e -> p h e"), stg[j][:, 0:512].rearrange("p (h e) -> p h e", h=4), [k_stg[j]], TKn("RV%d" % l))
                i = ring_load([(wsrc(Wi, 0, 8, O_RG, 512), 0, 8, 512)], [kWi])
                v = slot_view(i, 0, 8, 512)
                for c in range(4):
                    yb = [4, 5] if nxt("yb", 2) == 0 else [6, 7]
                    mm_group(yb, 128, lambda k: v[:, k, c * 128:(c + 1) * 128], rhs, 8, nsub, [ring_tk[i], k_hT])
                    evac_store(ps2(yb, 128, n), AF.Silu, RGS[l][c], TKn("RGS%d" % l), t0)
                i = ring_load([(wsrc(Wi, 0, 8, O_PU, 512), 0, 8, 512)], [kWi])
                v = slot_view(i, 0, 8, 512)
                for c in range(4):
                    yb = [4, 5] if nxt("yb", 2) == 0 else [6, 7]
                    mm_group(yb, 128, lambda k: v[:, k, c * 128:(c + 1) * 128], rhs, 8, nsub, [ring_tk[i], k_hT])
                    j = nxt("tmp", 2)
                    for (ap_, tk_, c0, w) in ps2(yb, 128, n):
                        T.op("act", lambda E, ap_=ap_, c0=c0, w=w, j=j: E.copy(out=tmp[j][:, c0:c0 + w], in_=ap_), [tk_], [k_tmp[j]])
                    T.dma("pool", PUT[l][c][:, t0:t0 + n], tmp[j][:, 0:n], [k_tmp[j]], TKn("PUT%d" % l))
                if g == 1:
                    i = ring_load([(wsrc(Wi, 0, 8, O_KVC, 160), 0, 8, 160)], [kWi])
                    v = slot_view(i, 0, 8, 160)
                    for blk in range(n // 128):
                        b = 4 + nxt("yb", 4)
                        for k in range(8):
                            T.op("pe", lambda E, k=k, b=b, blk=blk: E.matmul(psb[b][:, 0:160], lhsT=hT[:, k, blk * 128:(blk + 1) * 128], rhs=v[:, k, :], start=(k == 0), stop=(k == 7)),
                                 [ring_tk[i], k_hT[k]], [ps_tk[b]], inc=(k == 7))
                        j = nxt("tmp", 2)
                        T.op("act", lambda E, j=j, b=b: E.copy(out=tmp[j][:, 0:160], in_=psb[b][:, 0:160]), [ps_tk[b]], [k_tmp[j]])
                        T.op("act", lambda E, j=j: E.activation(out=tmp[j][:, 256:384], in_=tmp[j][:, 0:128], func=AF.Square, accum_out=tmp[j][:, 512:513]),
                             [k_tmp[j]], [k_tmp[j]])
                        T.op("act", lambda E, j=j: E.activation(out=tmp[j][:, 512:513], in_=tmp[j][:, 512:513], func=AF.Sqrt, scale=1.0 / 128, bias=eps_t[:, 0:1]),
                             [k_tmp[j], k_const], [k_tmp[j]])
                        T.op("dve", lambda E, j=j: E.reciprocal(out=tmp[j][:, 512:513], in_=tmp[j][:, 512:513]), [k_tmp[j]], [k_tmp[j]])
                        T.op("dve", lambda E, j=j: E.scalar_tensor_tensor(out=tmp[j][:, 256:384], in0=tmp[j][:, 0:128], scalar=tmp[j][:, 512:513],
                                                                          in1=kvn_bc[:, l * 128:(l + 1) * 128], op0=ALU.mult, op1=ALU.mult), [k_tmp[j], k_const], [k_tmp[j]])
                        sq_, r_ = divmod(blk, 2)
                        T.dma("pool", ockv[sq_, l, r_ * 128:(r_ + 1) * 128, :], tmp[j][:, 256:384], [k_tmp[j]], k_out[2])
                        T.dma("pool", okr[sq_, l, r_ * 128:(r_ + 1) * 128, :], tmp[j][:, 128:160], [k_tmp[j]], k_out[3])

            def rope_out(M, ba, bb, tcos, tsin, kcos, ksin, n, dst2, kdst, t0, p0):
                j = nxt("stg", 3)
                for s_ in range(n // 512):
                    sl = slice(s_ * 512, (s_ + 1) * 512)
                    t1 = nxt("tmp", 2)
                    T.op("dve", lambda E, s_=s_, t1=t1, sl=sl: E.tensor_tensor(out=tmp[t1][p0:p0 + M, 0:512], in0=psb[ba[s_]][p0:p0 + M, :], in1=tcos[p0:p0 + M, sl], op=ALU.mult),
                         [ps_tk[ba[s_]], kcos], [k_tmp[t1]])
                    T.op("dve", lambda E, s_=s_, t1=t1, sl=sl: E.tensor_tensor(out=tmp[t1][p0:p0 + M, 512:1024], in0=psb[bb[s_]][p0:p0 + M, :], in1=tsin[p0:p0 + M, sl], op=ALU.mult),
                         [ps_tk[bb[s_]], ksin], [k_tmp[t1]])
                    T.op("pool", lambda E, t1=t1, sl=sl, j=j: E.tensor_tensor(out=stg[j][p0:p0 + M, sl], in0=tmp[t1][p0:p0 + M, 0:512], in1=tmp[t1][p0:p0 + M, 512:1024], op=ALU.add),
                         [k_tmp[t1]], [k_stg[j]])
                T.dma("pool", dst2[:, t0:t0 + n], stg[j][p0:p0 + M, 0:n], [k_stg[j]], kdst)

            def merge(l, t0, n, g):
                nsub = n // 512
                kin = [TKn("OA%d" % l), TKn("OB%d" % l), TKn("OC%d" % l)]
                T.dma("sp", aT[:, 0:4, 0:n], OA[l][:, t0:t0 + n].rearrange("(c p) t -> p c t", p=128), [kin[0]], k_aT)
                T.dma("sp", aT[:, 4:8, 0:n], OB[l][:, :, t0:t0 + n].rearrange("c p t -> p c t"), [kin[1]], k_aT)
                T.dma("sp", aT[:, 8:12, 0:n], OC[l][:, :, t0:t0 + n].rearrange("c p t -> p c t"), [kin[2]], k_aT)
                Wbs = [ba_b[l], br_b[l], bp_b[l]]
                kWs = [tk["ba_%d" % l], tk["br_%d" % l], tk["bp_%d" % l]]
                for u in range(4):
                    i = ring_load([(wsrc(Wbs[b_], 0, 4, u * 256, 256), b_ * 1024, 4, 256) for b_ in range(3)], kWs)
                    for jj in range(2):
                        dc = u * 2 + jj
                        for b_ in range(3):
                            T.dma("sp", gate[b_][:, 0:n], SG[l][b_ * 8 + dc][:, t0:t0 + n], [TKn("SG%d" % l)], k_gate[b_])
                        accj = None
                        for b_ in range(3):
                            v = slot_view(i, b_ * 1024, 4, 256)
                            yb = [4, 5] if nxt("yb", 2) == 0 else [6, 7]
                            mm_group(yb, 128, lambda k: v[:, k, jj * 128:(jj + 1) * 128], lambda k, s_: aT[:, b_ * 4 + k, s_ * 512:(s_ + 1) * 512], 4, nsub, [ring_tk[i], k_aT])
                            if b_ == 0:
                                accj = nxt("tmp", 2)
                                for (ap_, tk_, c0, w) in ps2(yb, 128, n):
                                    T.op("dve", lambda E, ap_=ap_, c0=c0, w=w: E.tensor_tensor(out=tmp[accj][:, c0:c0 + w], in0=ap_, in1=gate[0][:, c0:c0 + w], op=ALU.mult),
                                         [tk_, k_gate[0]], [k_tmp[accj]])
                            else:
                                o_ = 1 - accj
                                for (ap_, tk_, c0, w) in ps2(yb, 128, n):
                                    T.op("dve", lambda E, ap_=ap_, c0=c0, w=w, b_=b_: E.tensor_tensor(out=tmp[o_][:, c0:c0 + w], in0=ap_, in1=gate[b_][:, c0:c0 + w], op=ALU.mult),
                                         [tk_, k_gate[b_]], [k_tmp[o_]])
                                if b_ == 1:
                                    T.op("pool", lambda E: E.tensor_tensor(out=tmp[accj][:, 0:n], in0=tmp[accj][:, 0:n], in1=tmp[o_][:, 0:n], op=ALU.add),
                                         [k_tmp[o_], k_tmp[accj]], [k_tmp[accj]])
                                else:
                                    T.op("pool", lambda E, dc=dc: E.tensor_tensor(out=hT[:, dc, 0:n], in0=tmp[accj][:, 0:n], in1=tmp[o_][:, 0:n], op=ALU.add),
                                         [k_tmp[o_], k_tmp[accj]], [k_hT[dc]])
                load_x(t0, n, g)
                Wo = out_b[l]
                kWo = tk["out_%d" % l]
                for u in range(2):
                    i = ring_load([(wsrc(Wo, 0, 8, u * 512, 512), 0, 8, 512)], [kWo])
                    v = slot_view(i, 0, 8, 512)
                    for c in range(4):
                        dc = u * 4 + c
                        yb = [4, 5] if nxt("yb", 2) == 0 else [6, 7]
                        mm_group(yb, 128, lambda k: v[:, k, c * 128:(c + 1) * 128], lambda k, s_: hT[:, k, s_ * 512:(s_ + 1) * 512], 8, nsub, [ring_tk[i], k_hT])
                        for (ap_, tk_, c0, w) in ps2(yb, 128, n):
                            T.op("act", lambda E, ap_=ap_, c0=c0, w=w, dc=dc: E.copy(out=yT[:, dc, c0:c0 + w], in_=ap_), [tk_], [k_yT[dc]])
                post_resid(n, g, 1)

            gate = [sbx("gate%d" % j, [128, 1024], BF16) for j in range(3)]
            k_gate = [Tk("gate%d" % j) for j in range(3)]

            for ti, (t0, n, g) in enumerate(STS):
                if phase == 1:
                    if l == 0 or ti == 0:
                        load_x(t0, n, g)
                    ffn(n, g, 0, 0)
                    store_fm(X1T[l], TKn("X1T%d" % l), t0, n)
                    prenorm(n, g, 1)
                    pre = None
                    if l > 0 and ti + 1 < len(STS):
                        pre = (lambda nx=STS[ti + 1]: load_x(*nx))
                    proj(l, t0, n, g, pre)
                else:
                    merge(l, t0, n, g)
                    ffn(n, g, 2, 1)
                    if l == L - 1:
                        store_final(t0, n, g)
                    else:
                        store_fm(XT1, TKn("XT1"), t0, n)
            T.barrier(list(tk.values()) + ring_tk + k_out)
            T.end_phase()
            T.end_phase()

        SCALE = 96.0 ** -0.5

        def run_attention(l, esx):
            def sbx(name, shape, dt):
                return esx.enter_context(nc.sbuf_tensor(name + "_a%d" % l, list(shape), dt))
            NKMAX = TL + PAST
            ckvT = sbx("ckvT", [128, NKMAX], BF16)
            KhT = [sbx("KhT%d" % j, [96, NKMAX], BF16) for j in range(2)]
            Vall = sbx("Vall", [128, 36 * 8 * 65], BF16)
            wukv = sbx("wukv", [128, 1024], BF16)
            wuq = sbx("wuq", [128, 2 * 768], BF16)
            wuqr = sbx("wuqr", [128, 2 * 768], BF16)
            qcn = [sbx("qcn%d" % j, [128, 2 * 512], BF16) for j in range(2)]
            QhT = [sbx("QhT%d" % j, [96, 512], BF16) for j in range(2)]
            PT = [sbx("PT%d" % j, [128, 512], BF16) for j in range(3)]
            tabc = [sbx("tabc%d" % j, [96, 512], F32) for j in range(2)]
            tabs = [sbx("tabs%d" % j, [96, 512], F32) for j in range(2)]
            Of = [sbx("Of%d" % j, [65, 512], F32) for j in range(2)]
            rec = [sbx("rec%d" % j, [65, 512], F32) for j in range(2)]
            oab = [sbx("oab%d" % j, [64, 512], BF16) for j in range(2)]
            rt1 = [sbx("rt1%d" % j, [96, 1024], F32) for j in range(2)]
            cst = sbx("cst", [128, 4 * 128], F32)
            cst2 = sbx("cst2", [128, 4 * 96], F32)
            k_ckvT, k_V, k_w, k_cst = Tk("ckvT"), Tk("V"), Tk("w"), Tk("cst")
            k_Kh = [Tk("Kh0"), Tk("Kh1")]
            k_qcn = [Tk("qcn0"), Tk("qcn1")]
            k_Qh = [Tk("Qh0"), Tk("Qh1")]
            k_PT = [Tk("PT%d" % j) for j in range(3)]
            k_tab = [Tk("tb0"), Tk("tb1")]
            k_Of = [Tk("Of0"), Tk("Of1")]
            k_rec = [Tk("rec0"), Tk("rec1")]
            k_oab = [Tk("oab0"), Tk("oab1")]
            k_rt1 = [Tk("rt10"), Tk("rt11")]
            T.dma("sp", wukv[:], ukv_b[l][:, :], [tk["ukv_%d" % l]], k_w)
            T.dma("sp", wuq[:].rearrange("p (k n) -> p k n", k=2), uq_b[l].rearrange("(k p) n -> p k n", p=128), [tk["uq_%d" % l]], k_w)
            T.dma("sp", wuqr[:].rearrange("p (k n) -> p k n", k=2), uqr_b[l].rearrange("(k p) n -> p k n", p=128), [tk["uqr_%d" % l]], k_w)
            wuq3 = wuq[:].rearrange("p (k n) -> p k n", k=2)
            wuqr3 = wuqr[:].rearrange("p (k n) -> p k n", k=2)
            V4 = Vall[:].rearrange("p (t h e) -> p t h e", h=8, e=65)
            T.op("dve", lambda E: E.memset(Vall[:], 1.0), [], [k_V])
            qi = [0]
            seqs = [(0, TL, True), (TL, 256, False), (TL + 256, 256, False)]
            for (s0, ns, has_cache) in seqs:
                nk = ns + (PAST if has_cache else 0)
                nkt = nk // 128
                T.dma("sp", ckvT[:, 0:ns], CKVT[l][:, s0:s0 + ns], [TKn("CKVT%d" % l)], k_ckvT)
                for j in range(2):
                    T.dma("sp", KhT[j][64:96, 0:ns], KRT[l][:, s0:s0 + ns], [TKn("KRT%d" % l)], k_Kh[j])
                if has_cache:
                    T.dma("sp", cst[:].rearrange("p (b f) -> p b f", b=4), cckv[l].rearrange("(b p) f -> p b f", p=128), [], k_cst)
                    T.op("dve", lambda E: E.memset(cst2[:], 0.0), [], [k_cst])
                    T.dma("sp", cst2[:].rearrange("p (b f) -> p b f", b=4)[:, :, 64:96], ckr[l].rearrange("(b p) f -> p b f", p=128), [], k_cst)
                    for b_ in range(4):
                        T.op("pe", lambda E, b_=b_: E.transpose(psb[0][:, b_ * 128:(b_ + 1) * 128], cst[:, b_ * 128:(b_ + 1) * 128], ident[:]), [k_cst, k_const], [ps_tk[0]])
                        T.op("pe", lambda E, b_=b_: E.transpose(psb[1][0:96, b_ * 128:(b_ + 1) * 128], cst2[:, b_ * 96:(b_ + 1) * 96], ident[:]), [k_cst, k_const], [ps_tk[1]])
                    T.op("act", lambda E: E.copy(out=ckvT[:, ns:ns + 512], in_=psb[0][:, :]), [ps_tk[0]], [k_ckvT])
                    for j in range(2):
                        T.op("act", lambda E, j=j: E.copy(out=KhT[j][64:96, ns:ns + 512], in_=psb[1][64:96, :]), [ps_tk[1]], [k_Kh[j]])
                wv = wukv[:].rearrange("p (h c) -> p h c", h=8)[:, :, 64:128]
                for kt in range(nkt):
                    b = kt % 2
                    T.op("pe", lambda E, kt=kt, b=b: E.matmul(psb[b][:, :].rearrange("p (h e) -> p h e", h=8), lhsT=ckvT[:, kt * 128:(kt + 1) * 128], rhs=wv, start=True, stop=True),
                         [k_ckvT, k_w], [ps_tk[b]])
                    T.op("act" if kt % 2 == 0 else "dve", lambda E, kt=kt, b=b: (E.copy if kt % 2 == 0 else E.tensor_copy)(
                        out=V4[:, kt, :, 0:64], in_=psb[b][:, :].rearrange("p (h e) -> p h e", h=8)), [ps_tk[b]], [k_V])
                nq = ns
                qn = min(512, nq)

                def kproj(h):
                    kj = h % 2
                    for blk in range((nk + 511) // 512):
                        w = min(512, nk - blk * 512)
                        b = blk % 2
                        T.op("pe", lambda E, blk=blk, w=w, b=b, h=h: E.matmul(psb[b][0:64, 0:w], lhsT=wukv[:, h * 128:h * 128 + 64], rhs=ckvT[:, blk * 512:blk * 512 + w], start=True, stop=True),
                             [k_w, k_ckvT], [ps_tk[b]])
                        T.op("act" if blk % 2 == 0 else "dve", lambda E, blk=blk, w=w, b=b: (E.copy if blk % 2 == 0 else E.tensor_copy)(
                            out=KhT[kj][0:64, blk * 512:blk * 512 + w], in_=psb[b][0:64, 0:w]), [ps_tk[b]], [k_Kh[kj]])

                def prologue(h, qt):
                    t0 = s0 + qt * qn
                    j = qi[0] % 2
                    qi[0] += 1
                    T.dma("sp", qcn[j][:, :].rearrange("p (k n) -> p k n", k=2)[:, :, 0:qn], QCN[l][:, :, t0:t0 + qn].rearrange("k p t -> p k t"), [TKn("QCN%d" % l)], k_qcn[j])
                    T.dma("sp", tabc[j][64:96, 0:qn], c_cosM[64:96, t0:t0 + qn], [], k_tab[j])
                    T.dma("sp", tabs[j][64:96, 0:qn], c_sinM[64:96, t0:t0 + qn], [], k_tab[j])
                    q3 = qcn[j][:, :].rearrange("p (k n) -> p k n", k=2)
                    for k in range(2):
                        T.op("pe", lambda E, k=k: E.matmul(psb[0][0:96, 0:qn], lhsT=wuq3[:, k, h * 96:(h + 1) * 96], rhs=q3[:, k, 0:qn], start=(k == 0), stop=(k == 1)),
                             [k_w, k_qcn[j]], [ps_tk[0]], inc=(k == 1))
                    for k in range(2):
                        T.op("pe", lambda E, k=k: E.matmul(psb[1][0:96, 0:qn], lhsT=wuqr3[:, k, h * 96:(h + 1) * 96], rhs=q3[:, k, 0:qn], start=(k == 0), stop=(k == 1)),
                             [k_w, k_qcn[j]], [ps_tk[1]], inc=(k == 1))
                    T.op("dve", lambda E: E.tensor_copy(out=QhT[j][0:64, 0:qn], in_=psb[0][0:64, 0:qn]), [ps_tk[0]], [k_Qh[j]])
                    T.op("dve", lambda E: E.tensor_tensor(out=rt1[j][64:96, 0:qn], in0=psb[0][64:96, 0:qn], in1=tabc[j][64:96, 0:qn], op=ALU.mult), [ps_tk[0], k_tab[j]], [k_rt1[j]])
                    T.op("dve", lambda E: E.tensor_tensor(out=rt1[j][64:96, 512:512 + qn], in0=psb[1][64:96, 0:qn], in1=tabs[j][64:96, 0:qn], op=ALU.mult), [ps_tk[1], k_tab[j]], [k_rt1[j]])
                    T.op("pool", lambda E: E.tensor_tensor(out=QhT[j][64:96, 0:qn], in0=rt1[j][64:96, 0:qn], in1=rt1[j][64:96, 512:512 + qn], op=ALU.add), [k_rt1[j]], [k_Qh[j]])
                    return j

                def mainloop(h, qt, j, pending):
                    t0 = s0 + qt * qn
                    kj = h % 2
                    ob = 6 + j
                    SB = [2, 3, 4]
                    for it in range(nkt + 2):
                        if it < nkt:
                            sbk = SB[it % 3]
                            T.op("pe", lambda E, it=it, sbk=sbk: E.matmul(psb[sbk][:, 0:qn], lhsT=KhT[kj][:, it * 128:(it + 1) * 128], rhs=QhT[j][:, 0:qn], start=True, stop=True),
                                 [k_Kh[kj], k_Qh[j]], [ps_tk[sbk]])
                        if 1 <= it <= nkt:
                            i1 = it - 1
                            sbk = SB[i1 % 3]
                            T.op("act", lambda E, i1=i1, sbk=sbk: E.activation(out=PT[i1 % 3][:, 0:qn], in_=psb[sbk][:, 0:qn], func=AF.Exp, scale=SCALE), [ps_tk[sbk]], [k_PT[i1 % 3]])
                        if it >= 2:
                            i2 = it - 2
                            T.op("pe", lambda E, i2=i2: E.matmul(psb[ob][0:65, 0:qn], lhsT=V4[:, i2, h, :], rhs=PT[i2 % 3][:, 0:qn], start=(i2 == 0), stop=(i2 == nkt - 1)),
                                 [k_V, k_PT[i2 % 3]], [ps_tk[ob]], inc=(i2 == nkt - 1))
                        if pending is not None and it == min(10, nkt):
                            pending()
                            pending = None
                    if pending is not None:
                        pending()
                    T.op("act", lambda E: E.copy(out=Of[j][:, 0:qn], in_=psb[ob][0:65, 0:qn]), [ps_tk[ob]], [k_Of[j]])
                    T.op("dve", lambda E: E.reciprocal(out=rec[j][64:65, 0:qn], in_=Of[j][64:65, 0:qn]), [k_Of[j]], [k_rec[j]])

                    def ep2():
                        T.op("pe", lambda E: E.matmul(psb[5][0:64, 0:qn], lhsT=ones_f[64:65, 0:64], rhs=rec[j][64:65, 0:qn], start=True, stop=True), [k_rec[j], k_const], [ps_tk[5]])
                        T.op("dve", lambda E: E.tensor_tensor(out=oab[j][:, 0:qn], in0=Of[j][0:64, 0:qn], in1=psb[5][0:64, 0:qn], op=ALU.mult), [k_Of[j], ps_tk[5]], [k_oab[j]])
                        T.dma("pool", OA[l][h * 64:(h + 1) * 64, t0:t0 + qn], oab[j][:, 0:qn], [k_oab[j]], TKn("OA%d" % l))
                    return ep2

                units = [(h, qt) for h in range(8) for qt in range(nq // qn)]
                kproj(0)
                jcur = prologue(*units[0])
                pend = None
                for idx, (h, qt) in enumerate(units):
                    jn = None
                    if idx + 1 < len(units):
                        h2, qt2 = units[idx + 1]
                        if h2 != h:
                            kproj(h2)
                        jn = prologue(h2, qt2)
                    pend = mainloop(h, qt, jcur, pend)
                    jcur = jn
                pend()
            T.barrier(list(tk.values()) + ring_tk + k_out)
            T.end_phase()

        def run_retention(l, esx):
            def sbx(name, shape, dt):
                return esx.enter_context(nc.sbuf_tensor(name + "_r%d" % l, list(shape), dt))
            kT = sbx("kT", [128, TL], BF16)
            qT = sbx("qT", [128, TL], BF16)
            vtm = sbx("vtm", [128, 32 * 128], BF16)
            kdf = sbx("kdf", [128, 32 * 128], BF16)
            kdb = sbx("kdb", [128, 32 * 128], BF16)
            Sf = sbx("Sf", [128, 33 * 128], BF16)
            Sb = sbx("Sb", [128, 33 * 128], BF16)
            S = [sbx("S%d" % j, [128, 128], F32) for j in range(2)]
            qdf = [sbx("qdf%d" % j, [128, 512], BF16) for j in range(2)]
            qdb = [sbx("qdb%d" % j, [128, 512], BF16) for j in range(2)]
            sd = [sbx("sd%d" % j, [128, 512], BF16) for j in range(2)]
            sq = [sbx("sqr%d" % j, [128, 512], BF16) for j in range(2)]
            rs = [sbx("rsr%d" % j, [128, 512], F32) for j in range(2)]
            tt = [sbx("ttr%d" % j, [128, 512], F32) for j in range(2)]
            rg = [sbx("rgr%d" % j, [128, 512], BF16) for j in range(2)]
            obt = [sbx("obr%d" % j, [128, 512], BF16) for j in range(2)]
            k_kT, k_qT, k_v, k_kdf, k_kdb, k_Sf, k_Sb = Tk("kT"), Tk("qT"), Tk("v"), Tk("kdf"), Tk("kdb"), Tk("Sf"), Tk("Sb")
            k_S = [Tk("S0"), Tk("S1")]
            k2 = {nm: [Tk(nm + "0"), Tk(nm + "1")] for nm in ["qdf", "qdb", "sd", "sq", "rs", "tt", "rg", "ob"]}
            RT = sbx("RT", [128, 4 * 1536], F32)
            RS = sbx("RS", [128, 4 * 4], F32)
            cret = sbx("cret", [128, 6 * 512], F32)
            ckj = sbx("ckj", [128, 2], F32)
            lg = sbx("lg", [128, L * 8], F32)
            tmpa = sbx("tmpa", [128, 512], F32)
            tmpb = sbx("tmpb", [128, 512], F32)
            k_cret, k_lg, k_ta, k_tb, k_RT, k_RS = Tk("cret"), Tk("lg"), Tk("ta"), Tk("tb"), Tk("RT"), Tk("RS")
            for i in range(6):
                T.dma("sp", cret[:, i * 512:(i + 1) * 512], c_ret[i], [], k_cret)
            T.dma("sp", ckj[:], c_kj[:, :], [], k_cret)
            T.dma("sp", lg[:], ret_decay.rearrange("l e -> (l e)").partition_broadcast(128), [], k_lg)
            T.op("act", lambda E: E.activation(out=lg[:], in_=lg[:], func=AF.Exp), [k_lg], [k_lg])
            T.op("dve", lambda E: E.tensor_scalar(out=lg[:], in0=lg[:], scalar1=-1.0, scalar2=None, op0=ALU.mult), [k_lg], [k_lg])
            DKS = 128.0 ** -0.5
            for h in range(4):
                lf = lg[:, l * 8 + h:l * 8 + h + 1]
                lb = lg[:, l * 8 + 4 + h:l * 8 + 4 + h + 1]
                o = h * 1536
                T.op("act", lambda E: E.activation(out=tmpa[:], in_=cret[:, 0:512], func=AF.Exp, scale=lf), [k_cret, k_lg], [k_ta])
                T.op("dve", lambda E: E.tensor_tensor(out=tmpa[:], in0=tmpa[:], in1=cret[:, 1024:1536], op=ALU.mult), [k_ta, k_cret], [k_ta])
                T.op("act", lambda E: E.activation(out=tmpb[:], in_=cret[:, 512:1024], func=AF.Exp, scale=lb), [k_cret, k_lg], [k_tb])
                T.op("dve", lambda E: E.tensor_tensor(out=tmpb[:], in0=tmpb[:], in1=cret[:, 1536:2048], op=ALU.mult), [k_tb, k_cret], [k_tb])
                T.op("dve", lambda E: E.tensor_tensor(out=RT[:, o:o + 512], in0=tmpa[:], in1=tmpb[:], op=ALU.add), [k_ta, k_tb], [k_RT])
                T.op("dve", lambda E: E.tensor_scalar(out=RT[:, o:o + 512], in0=RT[:, o:o + 512], scalar1=DKS, scalar2=None, op0=ALU.mult), [k_RT], [k_RT])
                T.op("act", lambda E: E.activation(out=RT[:, o + 512:o + 1024], in_=cret[:, 2048:2560], func=AF.Exp, scale=lf), [k_cret, k_lg], [k_RT])
                T.op("act", lambda E: E.activation(out=RT[:, o + 1024:o + 1536], in_=cret[:, 2560:3072], func=AF.Exp, scale=lb), [k_cret, k_lg], [k_RT])
                r = h * 4
                T.op("act", lambda E: E.activation(out=RS[:, r:r + 1], in_=ckj[:, 0:1], func=AF.Exp, scale=lf), [k_cret, k_lg], [k_RS])
                T.op("act", lambda E: E.activation(out=RS[:, r + 1:r + 2], in_=ckj[:, 1:2], func=AF.Exp, scale=lb), [k_cret, k_lg], [k_RS])
                T.op("dve", lambda E: E.tensor_scalar(out=RS[:, r:r + 2], in0=RS[:, r:r + 2], scalar1=DKS, scalar2=None, op0=ALU.mult), [k_RS], [k_RS])
                T.op("act", lambda E: E.activation(out=RS[:, r + 2:r + 3], in_=lf, func=AF.Exp, scale=128.0), [k_lg], [k_RS])
                T.op("act", lambda E: E.activation(out=RS[:, r + 3:r + 4], in_=lb, func=AF.Exp, scale=128.0), [k_lg], [k_RS])
            psT = psb[7][:, :].bitcast(BF16)
            if RET_STOP <= 1:
                T.barrier(); return
            gi = [0]
            seqs = [(0, TL, 0, None), (TL, 256, 1, 0), (TL + 256, 256, 1, 1)]
            for (s0, ns, is_p, pidx) in seqs:
                NC_ = ns // 128
                for h in range(4):
                    r = h * 4
                    o = h * 1536
                    T.dma("sp", kT[:, 0:ns], RKT[l][h][:, s0:s0 + ns], [TKn("RKT%d" % l)], k_kT)
                    T.dma("sp", qT[:, 0:ns], RQT[l][h][:, s0:s0 + ns], [TKn("RQT%d" % l)], k_qT)
                    T.dma("sp", vtm[:, 0:NC_ * 128].rearrange("p (c e) -> p c e", e=128), RV[l][h][:, s0 // 128:s0 // 128 + NC_, :], [TKn("RV%d" % l)], k_v)
                    if RET_STOP <= 1.5:
                        T.barrier(); return
                    for c0 in range(0, NC_, 8):
                        nb = min(8, NC_ - c0)
                        for c in range(c0, c0 + nb):
                            T.op("pe", lambda E, c=c, c0=c0: E.transpose(psT[:, (c - c0) * 128:(c - c0 + 1) * 128], kT[:, c * 128:(c + 1) * 128], ones_b[:] if False else identb[:]),
                                 [k_kT, k_const], [ps_tk[7]])
                        T.op("act", lambda E, c0=c0, nb=nb: E.activation(out=kdf[:, c0 * 128:(c0 + nb) * 128], in_=psT[:, 0:nb * 128], func=AF.Copy, scale=RS[:, r:r + 1]),
                             [ps_tk[7], k_RS], [k_kdf])
                        T.op("dve", lambda E, c0=c0, nb=nb: E.tensor_scalar(out=kdb[:, c0 * 128:(c0 + nb) * 128], in0=psT[:, 0:nb * 128], scalar1=RS[:, r + 1:r + 2], scalar2=None, op0=ALU.mult),
                             [ps_tk[7], k_RS, k_kdf], [k_kdb])
                    if RET_STOP <= 2:
                        T.barrier(); return
                    for dr in range(2):
                        if is_p:
                            T.op("dve", lambda E, dr=dr: E.memset(S[dr][:], 0.0), [], [k_S[dr]])
                        else:
                            T.dma("sp", S[dr][:], st0[l, dr, h], [], k_S[dr])
                    for ii in range(NC_):
                        for dr in range(2):
                            Sx, kSx, kd, kkd = (Sf, k_Sf, kdf, k_kdf) if dr == 0 else (Sb, k_Sb, kdb, k_kdb)
                            c = ii if dr == 0 else NC_ - 1 - ii
                            cd = RS[:, r + 2 + dr:r + 3 + dr]
                            T.op("act", lambda E, c=c, dr=dr, Sx=Sx: E.copy(out=Sx[:, c * 128:(c + 1) * 128], in_=S[dr][:]), [k_S[dr]], [kSx])
                            b = 2 * dr + (ii % 2)
                            T.op("pe", lambda E, c=c, b=b, kd=kd: E.matmul(psb[b][:, 0:128], lhsT=kd[:, c * 128:(c + 1) * 128], rhs=vtm[:, c * 128:(c + 1) * 128], start=True, stop=True),
                                 [kkd, k_v], [ps_tk[b]])
                            T.op("dve", lambda E, b=b, dr=dr, cd=cd: E.scalar_tensor_tensor(out=S[dr][:], in0=S[dr][:], scalar=cd, in1=psb[b][:, 0:128], op0=ALU.mult, op1=ALU.add),
                                 [k_S[dr], ps_tk[b], k_RS], [k_S[dr]])
                    if is_p:
                        for dr in range(2):
                            T.dma("pool", ost[pidx, l, dr, h], S[dr][:], [k_S[dr]], k_out[4])
                    if RET_STOP <= 3:
                        T.barrier(); return
                    for g0 in range(0, NC_, 4):
                        ng = min(4, NC_ - g0)
                        w = ng * 128
                        j = gi[0] % 2
                        gi[0] += 1
                        tok = slice(g0 * 128, g0 * 128 + w)
                        T.op("dve", lambda E, j=j: E.tensor_tensor(out=qdf[j][:, 0:w], in0=qT[:, tok], in1=RT[:, o + 512:o + 512 + w], op=ALU.mult), [k_qT, k_RT], [k2["qdf"][j]])
                        T.op("pool", lambda E, j=j: E.tensor_tensor(out=qdb[j][:, 0:w], in0=qT[:, tok], in1=RT[:, o + 1024:o + 1024 + w], op=ALU.mult), [k_qT, k_RT], [k2["qdb"][j]])
                        sbk = 2 + j
                        for ci in range(ng):
                            c = g0 + ci
                            T.op("pe", lambda E, c=c, ci=ci: E.matmul(psb[sbk][:, ci * 128:(ci + 1) * 128], lhsT=kT[:, c * 128:(c + 1) * 128], rhs=qT[:, c * 128:(c + 1) * 128], start=True, stop=True),
                                 [k_kT, k_qT], [ps_tk[sbk]])
                        T.op("dve", lambda E, j=j: E.tensor_tensor(out=sd[j][:, 0:w], in0=psb[sbk][:, 0:w], in1=RT[:, o:o + w], op=ALU.mult), [ps_tk[sbk], k_RT], [k2["sd"][j]])
                        obk = 4 + j
                        for ci in range(ng):
                            c = g0 + ci
                            cs = slice(ci * 128, (ci + 1) * 128)
                            T.op("pe", lambda E, c=c, cs=cs: E.matmul(psb[obk][:, cs], lhsT=vtm[:, c * 128:(c + 1) * 128], rhs=sd[j][:, cs], start=True, stop=False), [k_v, k2["sd"][j]], [ps_tk[obk]], inc=False)
                            T.op("pe", lambda E, c=c, cs=cs: E.matmul(psb[obk][:, cs], lhsT=Sf[:, c * 128:(c + 1) * 128], rhs=qdf[j][:, cs], start=False, stop=False), [k_Sf, k2["qdf"][j]], [ps_tk[obk]], inc=False)
                            T.op("pe", lambda E, c=c, cs=cs: E.matmul(psb[obk][:, cs], lhsT=Sb[:, c * 128:(c + 1) * 128], rhs=qdb[j][:, cs], start=False, stop=True), [k_Sb, k2["qdb"][j]], [ps_tk[obk]], inc=True)
                        T.op("act", lambda E, j=j: E.activation(out=sq[j][:, 0:w], in_=psb[obk][:, 0:w], func=AF.Square), [ps_tk[obk]], [k2["sq"][j]])
                        T.op("pe", lambda E, j=j: E.matmul(psb[6][:, 0:w], lhsT=ones_b[:], rhs=sq[j][:, 0:w], start=True, stop=True), [k2["sq"][j], k_const], [ps_tk[6]])
                        T.op("act", lambda E, j=j: E.activation(out=rs[j][:, 0:w], in_=psb[6][:, 0:w], func=AF.Sqrt, scale=1.0 / 128, bias=eps_t[:, 0:1]), [ps_tk[6], k_const], [k2["rs"][j]])
                        T.op("dve", lambda E, j=j: E.reciprocal(out=rs[j][:, 0:w], in_=rs[j][:, 0:w]), [k2["rs"][j]], [k2["rs"][j]])
                        T.op("dve", lambda E, j=j: E.tensor_tensor(out=tt[j][:, 0:w], in0=psb[obk][:, 0:w], in1=rs[j][:, 0:w], op=ALU.mult), [ps_tk[obk], k2["rs"][j]], [k2["tt"][j]])
                        T.dma("sp", rg[j][:, 0:w], RGS[l][h][:, s0 + g0 * 128:s0 + g0 * 128 + w], [TKn("RGS%d" % l)], k2["rg"][j])
                        T.op("pool", lambda E, j=j: E.tensor_tensor(out=obt[j][:, 0:w], in0=tt[j][:, 0:w], in1=rg[j][:, 0:w], op=ALU.mult), [k2["tt"][j], k2["rg"][j]], [k2["ob"][j]])
                        T.dma("pool", OB[l][h][:, s0 + g0 * 128:s0 + g0 * 128 + w], obt[j][:, 0:w], [k2["ob"][j]], TKn("OB%d" % l))
            T.barrier(list(tk.values()) + ring_tk + k_out)
            T.end_phase()

        identb = sb("identb", [128, 128], BF16)
        T.op("dve", lambda E: E.tensor_copy(out=identb[:], in_=ident[:]), [k_const], [k_const])

        def run_pool(l, esx):
            def sbx(name, shape, dt):
                return esx.enter_context(nc.sbuf_tensor(name + "_p%d" % l, list(shape), dt))
            W_ = 1024 + 16
            U = [sbx("U%d" % j, [128, W_], F32) for j in range(2)]
            A2 = [sbx("A2%d" % j, [128, W_], F32) for j in range(2)]
            A4 = [sbx("A4%d" % j, [128, W_], F32) for j in range(2)]
            rc = [sbx("rc%d" % j, [128, 1024], F32) for j in range(2)]
            yb_ = [sbx("ypb%d" % j, [128, 1024], BF16) for j in range(2)]
            oc = [sbx("ocb%d" % j, [128, 1024], BF16) for j in range(2)]
            pw = sbx("pw", [128, 4 * 128], BF16)
            kk = {nm: [Tk(nm + "0"), Tk(nm + "1")] for nm in ["U", "A2", "A4", "rc", "y", "oc"]}
            k_pw = Tk("pw")
            T.dma("sp", pw[:].rearrange("p (g d) -> p g d", g=4), pw_b[l].rearrange("(g p) d -> p g d", p=128), [tk["pw_%d" % l]], k_pw)
            it = [0]
            for (s0, ns) in [(0, TL), (TL, 256), (TL + 256, 256)]:
                for a0 in range(0, ns, 1024):
                    n = min(1024, ns - a0)
                    for g in range(4):
                        j = it[0] % 2
                        it[0] += 1
                        lo = max(a0 - 8, 0)
                        hi = min(a0 + n + 8, ns)
                        T.op("pool", lambda E, j=j: E.memset(U[j][:, :], 0.0), [], [kk["U"][j]])
                        T.dma("sp", U[j][:, 8 + lo - a0:8 + hi - a0], PUT[l][g][:, s0 + lo:s0 + hi], [TKn("PUT%d" % l)], kk["U"][j])
                        T.dma("sp", rc[j][:, 0:n], c_rcnt[g:g + 1, s0 + a0:s0 + a0 + n].partition_broadcast(128), [], kk["rc"][j])
                        m = n + 16
                        T.op("dve", lambda E, j=j: E.tensor_tensor(out=A2[j][:, 1:m], in0=U[j][:, 0:m - 1], in1=U[j][:, 1:m], op=ALU.add), [kk["U"][j]], [kk["A2"][j]])
                        cur, kcur = A2[j], kk["A2"][j]
                        if g >= 1:
                            T.op("dve", lambda E, j=j: E.tensor_tensor(out=A4[j][:, 2:m - 1], in0=A2[j][:, 1:m - 2], in1=A2[j][:, 3:m], op=ALU.add), [kk["A2"][j]], [kk["A4"][j]])
                            cur, kcur = A4[j], kk["A4"][j]
                        if g >= 2:
                            T.op("dve", lambda E, j=j: E.tensor_tensor(out=A2[j][:, 4:m - 3], in0=A4[j][:, 2:m - 5], in1=A4[j][:, 6:m - 1], op=ALU.add), [kk["A4"][j], kk["A2"][j]], [kk["A2"][j]])
                            cur, kcur = A2[j], kk["A2"][j]
                        if g >= 3:
                            T.op("dve", lambda E, j=j: E.tensor_tensor(out=A4[j][:, 8:m - 7], in0=A2[j][:, 4:m - 11], in1=A2[j][:, 12:m - 3], op=ALU.add), [kk["A2"][j], kk["A4"][j]], [kk["A4"][j]])
                            cur, kcur = A4[j], kk["A4"][j]
                        T.op("dve", lambda E, j=j, cur=cur: E.tensor_tensor(out=rc[j][:, 0:n], in0=cur[:, 8:8 + n], in1=rc[j][:, 0:n], op=ALU.mult), [kcur, kk["rc"][j]], [kk["rc"][j]])
                        T.op("dve", lambda E, j=j: E.tensor_tensor(out=yb_[j][:, 0:n], in0=rc[j][:, 0:n], in1=U[j][:, 8:8 + n], op=ALU.subtract), [kk["rc"][j], kk["U"][j]], [kk["y"][j]])
                        for s_ in range((n + 511) // 512):
                            w = min(512, n - s_ * 512)
                            b = 2 * j + s_
                            T.op("pe", lambda E, s_=s_, w=w, b=b, g=g, j=j: E.matmul(psb[b][:, 0:w], lhsT=pw[:, g * 128:(g + 1) * 128], rhs=yb_[j][:, s_ * 512:s_ * 512 + w], start=True, stop=True),
                                 [k_pw, kk["y"][j]], [ps_tk[b]])
                            T.op("act", lambda E, s_=s_, w=w, b=b, g=g, j=j: E.activation(out=oc[j][:, s_ * 512:s_ * 512 + w], in_=psb[b][:, 0:w], func=AF.Copy, scale=vt(l, V_PS + g, 1)),
                                 [ps_tk[b], k_VT], [kk["oc"][j]])
                        T.dma("pool", OC[l][g][:, s0 + a0:s0 + a0 + n], oc[j][:, 0:n], [kk["oc"][j]], TKn("OC%d" % l))
            T.barrier(list(tk.values()) + ring_tk + k_out)
            T.end_phase()

        stop = debug_stop
        done = False
        for l in range(L):
            if stop == "setup":
                break
            with ExitStack() as esx:
                run_rowlocal(l, 1, esx)
            if stop == ("p1", l):
                break
            with ExitStack() as esx:
                run_attention(l, esx)
            if stop == ("att", l):
                break
            with ExitStack() as esx:
                run_retention(l, esx)
            if stop == ("ret", l):
                break
            with ExitStack() as esx:
                run_pool(l, esx)
            if stop == ("p2", l):
                break
            with ExitStack() as esx:
                run_rowlocal(l, 3, esx)
            if stop == ("p3", l):
                break
        if stop is not None and stop != "setup":
            if stop == ("p3", 0):
                dd = dout("dbg_XT1", [8, 128, TT])
                kd2 = Tk("dbg2")
                for i in range(8):
                    T.dma("pool", dd[i], XT1[i], [TKn("XT1")], kd2)
            kd = Tk("dbg")
            for nm, src in [("OA", OA[0]), ("OB", OB[0].rearrange("c p t -> (c p) t")), ("OC", OC[0].rearrange("c p t -> (c p) t"))]:
                dd = dout("dbg_" + nm, [512, TT])
                for i in range(4):
                    T.dma("pool", dd[i * 128:(i + 1) * 128, :], src[i * 128:(i + 1) * 128, :], [TKn("OA0"), TKn("OB0"), TKn("OC0")], kd)
        T.barrier(list(tk.values()) + ring_tk + k_out)
        build_program.ninst = T.ninst
    return nc


def _tables():
    inv8 = 10000.0 ** (-np.arange(8, dtype=np.float32) / 8)
    inv64 = 10000.0 ** (-np.arange(64, dtype=np.float32) / 64)
    t = np.arange(TL)
    row = (t // 64).astype(np.float32)
    col = (t % 64).astype(np.float32)
    cosM = np.ones((128, TT), np.float32)
    sinM = np.zeros((128, TT), np.float32)
    for p in range(128):
        r = p % 32
        pos = row if r < 16 else col
        f = inv8[r % 8]
        sgn = -1.0 if (r % 16) < 8 else 1.0
        ang = (pos * f).astype(np.float32)
        cosM[p, :TL] = np.cos(ang)
        sinM[p, :TL] = sgn * np.sin(ang)
    cosR = np.ones((128, TT), np.float32)
    sinR = np.zeros((128, TT), np.float32)
    for p in range(128):
        f = inv64[p % 64]
        sgn = -1.0 if p < 64 else 1.0
        ang = (t.astype(np.float32) * f).astype(np.float32)
        cosR[p, :TL] = np.cos(ang)
        sinR[p, :TL] = sgn * np.sin(ang)
    j = np.arange(128)[:, None].astype(np.float32)
    i = np.arange(128)[None, :].astype(np.float32)
    dpos = np.maximum(i - j, 0)
    dneg = np.maximum(j - i, 0)
    triu = (i >= j).astype(np.float32)
    tril = (j >= i).astype(np.float32)
    idx1 = np.broadcast_to(i + 1, (128, 128))
    idx2 = np.broadcast_to(128 - i, (128, 128))
    cret = np.stack([np.tile(a, (1, 4)) for a in [dpos, dneg, triu, tril, idx1, idx2]]).astype(np.float32)
    kj = np.stack([127 - np.arange(128), np.arange(128)], axis=1).astype(np.float32)
    rcnt = np.zeros((4, TT), np.float32)
    for g, w in enumerate((2, 4, 8, 16)):
        for (s0, n) in [(0, TL), (TL, 256), (TL + 256, 256)]:
            tt_ = np.arange(n)
            lo = np.clip(tt_ - w // 2, 0, n)
            hi = np.clip(tt_ + w // 2, 0, n)
            rcnt[g, s0:s0 + n] = 1.0 / (hi - lo)
    return dict(c_ident=np.eye(128, dtype=np.float32), c_cosR=cosR, c_sinR=sinR, c_cosM=cosM, c_sinM=sinM,
                c_ret=cret, c_kj=kj, c_rcnt=rcnt)


_CACHE = {}


def kernel(x_prompt, x_sample, cache_mla_ckv, cache_mla_krope, state_ret, c, c_ctx,
           w_ada, b_ada, norm_pre, norm_post, ffn1_w_gu, ffn1_w_down, ffn2_w_gu, ffn2_w_down,
           w_in, mla_q_norm, mla_w_uq, mla_kv_norm, mla_w_ukv, ret_decay, pool_w, pool_scale,
           w_branch_attn, w_branch_ret, w_branch_pool, w_out, _debug_stop=None):
    f = lambda a: np.ascontiguousarray(np.asarray(a, dtype=np.float32))
    if "nc" not in _CACHE or _CACHE.get("stop") != _debug_stop:
        _CACHE["nc"] = build_program(_debug_stop)
        _CACHE["stop"] = _debug_stop
        _CACHE["tables"] = _tables()
    nc = _CACHE["nc"]
    tabs = _CACHE["tables"]
    vecs = np.concatenate([f(b_ada).reshape(L, 72, 128), f(norm_pre).reshape(L, 24, 128), f(norm_post).reshape(L, 24, 128),
                           f(mla_q_norm).reshape(L, 2, 128), f(mla_kv_norm).reshape(L, 1, 128), f(pool_scale).reshape(L, 4, 128)], axis=1)
    shared = dict(vecs=f(vecs), kvn_row=f(mla_kv_norm), ret_decay=f(ret_decay).reshape(L, 8), w_ada=f(w_ada),
                  ffn1_w_gu=f(ffn1_w_gu), ffn2_w_gu=f(ffn2_w_gu), ffn1_w_down=f(ffn1_w_down), ffn2_w_down=f(ffn2_w_down),
                  w_in=f(w_in), mla_w_uq=f(mla_w_uq), mla_w_ukv=f(mla_w_ukv), pool_w=f(pool_w).reshape(L, 512, 128),
                  w_branch_attn=f(w_branch_attn), w_branch_ret=f(w_branch_ret), w_branch_pool=f(w_branch_pool), w_out=f(w_out))
    shared.update(tabs)
    xs = f(x_sample)
    xpr = f(x_prompt)
    in_maps = []
    for core in range(NCORES):
        b = core % 4
        m = dict(shared)
        m["xl"] = xs[b]
        m["xp"] = xpr[2 * core:2 * core + 2].reshape(TP, D)
        m["cckv"] = f(cache_mla_ckv)[b]
        m["ckr"] = f(cache_mla_krope)[b]
        m["st0"] = f(state_ret)[b]
        m["cv"] = np.concatenate([f(c)[b].reshape(8, 128), f(c_ctx).reshape(8, 128)], axis=0)
        in_maps.append(m)
    res = run_bass_kernel_spmd(nc, in_maps, core_ids=list(range(NCORES)))
    R = res.results
    y_sample = np.stack([R[b]["yl"] for b in range(4)], axis=0)
    y_prompt = np.concatenate([R[cc]["yp"].reshape(2, 256, D) for cc in range(NCORES)], axis=0)
    n_ckv = np.concatenate([R[cc]["ockv"] for cc in range(NCORES)], axis=0)
    n_kr = np.concatenate([R[cc]["okr"] for cc in range(NCORES)], axis=0)
    n_st = np.concatenate([R[cc]["ost"] for cc in range(NCORES)], axis=0)
    return (y_prompt.astype(np.float32), y_sample.astype(np.float32), n_ckv.astype(np.float32),
            n_kr.astype(np.float32), n_st.astype(np.float32))
```
